# Optimizing a Trainium2 kernel written in Bass

```python
import jax, jax.numpy as jnp
from jax import lax
import numpy as np

D_MODEL = 1024
BATCH = 4
SEQ = 8192
DEPTH = 1

PLE_DIM = 256
CONV_CH = D_MODEL // 2
CONV_WIDTH = 31
HEAD_DIM = 64
N_HEADS = (D_MODEL - CONV_CH) // HEAD_DIM
N_KV = 2
HPG = N_HEADS // N_KV
NSA_W = N_HEADS * HEAD_DIM
KV_W = N_KV * HEAD_DIM
CMP_LEN = 32
CMP_STRIDE = 16
CMP_HIDDEN = 256
SLC_BLOCK = 64
SLC_TOPN = 16
WINDOW = 512
Q_BLOCK = 128
D_FF = 4 * D_MODEL
N_BRANCH = 3
COL_SIZES = [2 * CONV_CH, NSA_W] + [KV_W] * 6 + [N_BRANCH * N_HEADS]
D_IN = sum(COL_SIZES)
LN_EPS = 1e-5
NEG_INF = -1e30
FORCE_BONUS = 1e4

kernel_name = "hymba_conformer_nsa_deepnorm_layer"


def layer_norm(x, g, b):
    xf = x.astype(jnp.float32)
    mu = xf.mean(-1, keepdims=True)
    var = jnp.square(xf - mu).mean(-1, keepdims=True)
    return ((xf - mu) * lax.rsqrt(var + LN_EPS) * g.astype(jnp.float32) + b.astype(jnp.float32)).astype(x.dtype)


def conformer_conv(u, dw_w, dw_b, ln_g, ln_b):
    a, g = jnp.split(u, 2, axis=-1)
    v = a * jax.nn.sigmoid(g)
    y = lax.conv_general_dilated(v, dw_w, window_strides=(1,), padding=[(CONV_WIDTH - 1, 0)],
                                 dimension_numbers=('NWC', 'WIO', 'NWC'),
                                 feature_group_count=CONV_CH) + dw_b
    return jax.nn.silu(layer_norm(y, ln_g, ln_b))


def compress_blocks(k, pos_emb, w1, w2):
    B, S, G, Dh = k.shape
    n_cmp = (S - CMP_LEN) // CMP_STRIDE + 1
    idx = np.arange(n_cmp)[:, None] * CMP_STRIDE + np.arange(CMP_LEN)[None, :]
    blk = k[:, idx] + pos_emb[:, None, :]
    blk = blk.transpose(0, 3, 1, 2, 4).reshape(B, G, n_cmp, CMP_LEN * Dh)
    return jax.nn.silu(blk @ w1) @ w2


def cmp_to_slc_matrix(n_cmp, n_slc):
    c0 = np.arange(n_cmp) * CMP_STRIDE
    c1 = c0 + CMP_LEN - 1
    s0 = np.arange(n_slc) * SLC_BLOCK
    s1 = s0 + SLC_BLOCK - 1
    return ((c0[:, None] <= s1[None, :]) & (c1[:, None] >= s0[None, :])).astype(np.float32)


def masked_softmax(s, mask):
    return jax.nn.softmax(jnp.where(mask, s, NEG_INF), axis=-1)


def nsa_attention(q, kc, vc, ks, vs, kw, vw, gates):
    B, G, H, S, Dh = q.shape
    dt = q.dtype
    n_cmp = kc.shape[2]
    n_slc = S // SLC_BLOCK
    top_n = min(SLC_TOPN, n_slc)
    scale = HEAD_DIM ** -0.5
    m_overlap = jnp.asarray(cmp_to_slc_matrix(n_cmp, n_slc))
    cmp_end = jnp.arange(n_cmp) * CMP_STRIDE + CMP_LEN - 1
    blk_id = jnp.arange(n_slc)
    ks_blk = ks.reshape(B, G, n_slc, SLC_BLOCK, Dh)
    vs_blk = vs.reshape(B, G, n_slc, SLC_BLOCK, Dh)
    pad = ((0, 0), (0, 0), (WINDOW, 0), (0, 0))
    kw_pad = jnp.pad(kw, pad)
    vw_pad = jnp.pad(vw, pad)
    b_idx = jnp.arange(B)[:, None, None, None]
    g_idx = jnp.arange(G)[None, :, None, None]

    def block(qb):
        t0 = qb * Q_BLOCK
        qq = lax.dynamic_slice_in_dim(q, t0, Q_BLOCK, axis=3)
        gg = jax.nn.sigmoid(lax.dynamic_slice_in_dim(gates, t0, Q_BLOCK, axis=3).astype(jnp.float32))
        pos = t0 + jnp.arange(Q_BLOCK)
        s_c = jnp.einsum('bghqd,bgnd->bghqn', qq, kc).astype(jnp.float32) * scale
        p_c = masked_softmax(s_c, cmp_end[None, :] <= pos[:, None])
        p_c = jnp.where((pos >= CMP_LEN - 1)[:, None], p_c, 0.0)
        o_c = jnp.einsum('bghqn,bgnd->bghqd', p_c.astype(dt), vc)
        imp = p_c.sum(axis=2) @ m_overlap
        cur = pos // SLC_BLOCK
        forced = (blk_id[None, :] == 0) | (blk_id[None, :] == cur[:, None]) | (blk_id[None, :] == cur[:, None] - 1)
        valid_b = blk_id[None, :] * SLC_BLOCK <= pos[:, None]
        score = jnp.where(valid_b, imp + jnp.where(forced, FORCE_BONUS, 0.0), NEG_INF)
        _, sel = lax.top_k(score, top_n)
        k_sel = ks_blk[b_idx, g_idx, sel].reshape(B, G, Q_BLOCK, top_n * SLC_BLOCK, Dh)
        v_sel = vs_blk[b_idx, g_idx, sel].reshape(B, G, Q_BLOCK, top_n * SLC_BLOCK, Dh)
        kpos = (sel[..., None] * SLC_BLOCK + jnp.arange(SLC_BLOCK)).reshape(B, G, Q_BLOCK, top_n * SLC_BLOCK)
        s_s = jnp.einsum('bghqd,bgqkd->bghqk', qq, k_sel).astype(jnp.float32) * scale
        p_s = masked_softmax(s_s, (kpos <= pos[None, None, :, None])[:, :, None])
        o_s = jnp.einsum('bghqk,bgqkd->bghqd', p_s.astype(dt), v_sel)
        k_win = lax.dynamic_slice_in_dim(kw_pad, t0, WINDOW + Q_BLOCK, axis=2)
        v_win = lax.dynamic_slice_in_dim(vw_pad, t0, WINDOW + Q_BLOCK, axis=2)
        kpos_w = t0 - WINDOW + jnp.arange(WINDOW + Q_BLOCK)
        valid_w = (kpos_w[None, :] <= pos[:, None]) & (kpos_w[None, :] > pos[:, None] - WINDOW) & (kpos_w[None, :] >= 0)
        s_w = jnp.einsum('bghqd,bgkd->bghqk', qq, k_win).astype(jnp.float32) * scale
        p_w = masked_softmax(s_w, valid_w)
        o_w = jnp.einsum('bghqk,bgkd->bghqd', p_w.astype(dt), v_win)
        o = gg[..., 0:1] * o_c + gg[..., 1:2] * o_s + gg[..., 2:3] * o_w
        return o.astype(dt)

    out = lax.map(block, jnp.arange(S // Q_BLOCK))
    return out.transpose(1, 0, 4, 2, 3, 5).reshape(B, S, G * H * Dh)


def setup_inputs(seed: int = 0) -> dict:
    key = jax.random.key(seed)
    ks = iter(jax.random.split(key, 40))
    beta = (8 * DEPTH) ** -0.25
    f32 = jnp.float32

    def nrm(shape, scale):
        return jax.random.normal(next(ks), shape, f32) * scale

    def gain(shape):
        return 1.0 + nrm(shape, 0.02)

    L = DEPTH
    return {
        "x": nrm((BATCH, SEQ, D_MODEL), 1.0),
        "p": nrm((DEPTH, BATCH, SEQ, PLE_DIM), 1.0),
        "w_in": nrm((L, D_MODEL, D_IN), D_MODEL ** -0.5),
        "b_in": nrm((L, D_IN), 0.02),
        "conv_dw_w": nrm((L, CONV_WIDTH, 1, CONV_CH), CONV_WIDTH ** -0.5),
        "conv_dw_b": nrm((L, CONV_CH), 0.02),
        "conv_ln_g": gain((L, CONV_CH)),
        "conv_ln_b": nrm((L, CONV_CH), 0.02),
        "cmp_pos_k": nrm((L, CMP_LEN, HEAD_DIM), 0.1),
        "cmp_w1_k": nrm((L, CMP_LEN * HEAD_DIM, CMP_HIDDEN), (CMP_LEN * HEAD_DIM) ** -0.5),
        "cmp_w2_k": nrm((L, CMP_HIDDEN, HEAD_DIM), CMP_HIDDEN ** -0.5),
        "cmp_pos_v": nrm((L, CMP_LEN, HEAD_DIM), 0.1),
        "cmp_w1_v": nrm((L, CMP_LEN * HEAD_DIM, CMP_HIDDEN), (CMP_LEN * HEAD_DIM) ** -0.5),
        "cmp_w2_v": nrm((L, CMP_HIDDEN, HEAD_DIM), CMP_HIDDEN ** -0.5),
        "w_out": nrm((L, D_MODEL, D_MODEL), D_MODEL ** -0.5 * beta),
        "b_out": nrm((L, D_MODEL), 0.02),
        "ln1_g": gain((L, D_MODEL)),
        "ln1_b": nrm((L, D_MODEL), 0.02),
        "w_up": nrm((L, D_MODEL, D_FF), D_MODEL ** -0.5),
        "b_up": nrm((L, D_FF), 0.02),
        "w_down": nrm((L, D_FF, D_MODEL), D_FF ** -0.5 * beta),
        "b_down": nrm((L, D_MODEL), 0.02),
        "w_pe": nrm((L, PLE_DIM, D_MODEL), PLE_DIM ** -0.5 * beta),
        "w_pg": nrm((L, D_MODEL, D_MODEL), D_MODEL ** -0.5),
        "ln2_g": gain((L, D_MODEL)),
        "ln2_b": nrm((L, D_MODEL), 0.02),
    }


def reference(x, p, w_in, b_in, conv_dw_w, conv_dw_b, conv_ln_g, conv_ln_b,
              cmp_pos_k, cmp_w1_k, cmp_w2_k, cmp_pos_v, cmp_w1_v, cmp_w2_v,
              w_out, b_out, ln1_g, ln1_b, w_up, b_up, w_down, b_down,
              w_pe, w_pg, ln2_g, ln2_b):
    alpha = (2 * DEPTH) ** 0.25
    B, S, _ = x.shape
    split_at = np.cumsum(COL_SIZES)[:-1].tolist()
    for i in range(DEPTH):
        h = x @ w_in[i] + b_in[i]
        u_conv, q, kc, vc, ksl, vsl, kwn, vwn, gt = jnp.split(h, split_at, axis=-1)
        y_conv = conformer_conv(u_conv, conv_dw_w[i], conv_dw_b[i], conv_ln_g[i], conv_ln_b[i])
        q = q.reshape(B, S, N_KV, HPG, HEAD_DIM).transpose(0, 2, 3, 1, 4)
        to_kv = lambda t: t.reshape(B, S, N_KV, HEAD_DIM)
        kc_t = compress_blocks(to_kv(kc), cmp_pos_k[i], cmp_w1_k[i], cmp_w2_k[i])
        vc_t = compress_blocks(to_kv(vc), cmp_pos_v[i], cmp_w1_v[i], cmp_w2_v[i])
        heads = lambda t: to_kv(t).transpose(0, 2, 1, 3)
        gt = gt.reshape(B, S, N_KV, HPG, N_BRANCH).transpose(0, 2, 3, 1, 4)
        y_nsa = nsa_attention(q, kc_t, vc_t, heads(ksl), heads(vsl), heads(kwn), heads(vwn), gt)
        mix = jnp.concatenate([y_conv, y_nsa], axis=-1) @ w_out[i] + b_out[i]
        x = layer_norm(alpha * x + mix, ln1_g[i], ln1_b[i])
        ff = jnp.square(jax.nn.relu(x @ w_up[i] + b_up[i])) @ w_down[i] + b_down[i]
        ple = (p[i] @ w_pe[i]) * jax.nn.sigmoid(x @ w_pg[i])
        x = layer_norm(alpha * x + ff + ple, ln2_g[i], ln2_b[i])
    return x
```

```python
import contextlib
import numpy as np
import concourse.bass as bass
import concourse.mybir as mybir
from concourse.bass_utils import run_bass_kernel_spmd

F32 = mybir.dt.float32
BF16 = mybir.dt.bfloat16
AF = mybir.ActivationFunctionType
ALU = mybir.AluOpType

NCORES = 8
S = 8192
D = 1024
NT_ALL = 64
NSTEP = 32
DFF = 4096
ALPHA = 2.0 ** 0.25
EPS = 1e-5
MASKV = 30000.0
ENGS = ["pe", "act", "dve", "pool", "sp"]
SEM_LIMIT = 24000


class _Rec:
    def __init__(self):
        self.call = None

    def __getattr__(self, name):
        def f(*a, **k):
            self.call = (name, a, k)
            return self
        return f


class Prog:
    def __init__(self, nc):
        self.nc = nc
        self.lists = {e: [] for e in ENGS}
        self.cnt = {}
        self.sems = {}
        self.cur = {}
        self.epoch = {}
        self.waited = {e: {} for e in ENGS}
        self.last_w = {}
        self.readers = {}
        self._stack = []
        for e in ENGS:
            self._new_epoch("e_" + e)

    def _new_epoch(self, base):
        ep = self.epoch.get(base, -1) + 1
        self.epoch[base] = ep
        key = "%s_%d" % (base, ep)
        cm = self.nc.semaphore(key)
        h = cm.__enter__()
        self._stack.append(cm)
        self.sems[key] = h
        self.cnt[key] = 0
        self.cur[base] = key

    def close(self):
        for cm in reversed(self._stack):
            cm.__exit__(None, None, None)

    def _deps(self, eng, reads, writes):
        need = {}
        mine = "e_" + eng + "_"

        def add(tok, same_ok):
            sk, v = tok
            if same_ok and sk.startswith(mine):
                return
            if v > need.get(sk, 0):
                need[sk] = v

        for b in reads:
            lw = self.last_w.get(b)
            if lw is not None:
                add(lw, eng == "pe")
        for b in writes:
            lw = self.last_w.get(b)
            if lw is not None:
                add(lw, True)
            for sk, v in self.readers.get(b, {}).items():
                add((sk, v), True)
        waits = []
        w = self.waited[eng]
        for sk, v in need.items():
            if w.get(sk, 0) < v:
                w[sk] = v
                waits.append((sk, v))
        return waits

    def _record(self, eng, waits, fn, base, inc):
        if self.cnt[self.cur[base]] + inc > SEM_LIMIT:
            self._new_epoch(base)
        key = self.cur[base]
        self.cnt[key] += inc
        val = self.cnt[key]
        ws = [(self.sems[sk], v) for sk, v in waits]
        sh = self.sems[key]
        rec = _Rec()
        fn(rec)
        name, args, kwargs = rec.call

        def emit(h):
            for s, v in ws:
                h.wait_ge(s, v)
            getattr(h, name)(*args, **kwargs).then_inc(sh, inc)

        self.lists[eng].append(emit)
        return (key, val)

    def _mark(self, tok, reads, writes):
        for b in reads:
            r = self.readers.setdefault(b, {})
            if r.get(tok[0], 0) < tok[1]:
                r[tok[0]] = tok[1]
        for b in writes:
            self.last_w[b] = tok
            self.readers[b] = {}

    def op(self, eng, fn, reads=(), writes=()):
        waits = self._deps(eng, reads, writes)
        tok = self._record(eng, waits, fn, "e_" + eng, 1)
        self._mark(tok, reads, writes)

    def dma(self, queue, dsem, fn, reads=(), writes=()):
        base = "dm_" + writes[0]
        if base not in self.cur:
            self._new_epoch(base)
        waits = self._deps(queue, reads, writes)
        tok = self._record(queue, waits, fn, base, 16)
        self._mark(tok, reads, writes)

    def barrier(self):
        snap = [(k, v) for k, v in self.cnt.items() if v > 0]
        for e in ENGS:
            waits = []
            for sk, v in snap:
                if self.waited[e].get(sk, 0) < v and not sk.startswith("e_" + e + "_"):
                    self.waited[e][sk] = v
                    waits.append((self.sems[sk], v))
            if waits:
                def emit(h, waits=waits):
                    for s, v in waits:
                        h.wait_ge(s, v)
                self.lists[e].append(emit)
        self.last_w = {}
        self.readers = {}

    def final_wait(self, eng="sp"):
        waits = [(self.sems[sk], v) for sk, v in self.cnt.items()
                 if v > 0 and not sk.startswith("e_" + eng + "_")]

        def emit(h):
            for s, v in waits:
                h.wait_ge(s, v)
        self.lists[eng].append(emit)

    def replay(self):
        lists = self.lists
        with self.nc.Block() as block:
            @block.tensor
            def _(h):
                for f in lists["pe"]:
                    f(h)

            @block.scalar
            def _(h):
                for f in lists["act"]:
                    f(h)

            @block.vector
            def _(h):
                for f in lists["dve"]:
                    f(h)

            @block.gpsimd
            def _(h):
                for f in lists["pool"]:
                    f(h)

            @block.sync
            def _(h):
                for f in lists["sp"]:
                    f(h)


C_BQ = 0
C_BKV = 4
C_BA = 8
C_BG = 12
C_DWB = 16
C_LNG = 20
C_LNB = 24
C_HALO = 28
C_BUP = 32
C_DW = 64
NCOL = 64 + 124
R_BV = 0
R_BGT = 256
R_BOUT = 288
R_G1 = R_BOUT + 1024
R_B1 = R_G1 + 1024
R_BDN = R_B1 + 1024
R_G2 = R_BDN + 1024
R_B2 = R_G2 + 1024
NROW = R_B2 + 1024


def build_program(debug=False):
    nc = bass.Bass("TRN2", target_bir_lowering=False, dynamic_dma_scratch_size=4096)

    def din(name, shape, dt=F32):
        return nc.dram_tensor(name, list(shape), dt, kind="ExternalInput").ap()

    xT = din("xT", [128, 8, S])
    xTo = din("xTo", [NSTEP, 128, 8, 160])
    xo = din("xo", [NSTEP, 128, D])
    pTo = din("pTo", [NSTEP, 128, 2, 128])
    w_kv = din("w_kv", [128, 8, 768])
    w_own = din("w_own", [128, 8, 1568])
    w1k = din("w1k", [64, 32, 256])
    w1v = din("w1v", [64, 32, 256])
    w2k = din("w2k", [128, 2, 2, 128])
    w2v = din("w2v", [128, 2, 64])
    posk = din("posk", [64, 32])
    posv = din("posv", [64, 32])
    w_out = din("w_out", [128, 8, D])
    w_pg = din("w_pg", [128, 8, D])
    w_pe = din("w_pe", [128, 2, D])
    w_up = din("w_up", [128, 8, DFF])
    w_dn = din("w_dn", [128, 32, D])
    cols_d = din("cols", [128, NCOL])
    rows_d = din("rows", [128, NROW])
    mov_d = din("mov", [128, 4, 128])
    wmask_d = din("wmask", [128, 6, 128])
    dmask_d = din("dmask", [128, 2, 128])
    cmask_d = din("cmask", [128, 8, 2, 128])
    bonus_d = din("bonus", [NSTEP, 128, 128])
    ident_d = din("ident", [128, 128])
    out_d = nc.dram_tensor("out", [NSTEP, 128, D], F32, kind="ExternalOutput").ap()
    ystash = nc.dram_tensor("ystash", [NSTEP, 128, 8, 128], BF16, **({"kind": "ExternalOutput"} if debug else {})).ap()
    dbg = {}
    if debug:
        for nm, shp in (("ksT", [128, S]), ("kwT", [128, S]), ("vsw", [128, NT_ALL, 4, 65]), ("kct", [128, 512]),
                        ("vct", [128, 4, 2, 65]), ("kcT", [128, S])):
            dbg[nm] = nc.dram_tensor("dbg_" + nm, shp, BF16, kind="ExternalOutput").ap()

    wupb = nc.dram_tensor("wupb", [128, 8, DFF], BF16, **({"kind": "ExternalOutput"} if debug else {})).ap()
    wdnb = nc.dram_tensor("wdnb", [128, 32, D], BF16, **({"kind": "ExternalOutput"} if debug else {})).ap()

    P = Prog(nc)
    op = P.op
    dma = P.dma

    with contextlib.ExitStack() as top:
        def sbt(stack, name, shape, dt):
            return stack.enter_context(nc.sbuf_tensor("s_" + name, list(shape), dt))

        banks = [top.enter_context(nc.psum_tensor("bank%d" % i, [128, 512], F32)) for i in range(8)]

        cols = sbt(top, "cols", [128, NCOL], F32)
        rows = sbt(top, "rows", [128, NROW], F32)
        identf = sbt(top, "identf", [128, 128], F32)
        identb = sbt(top, "identb", [128, 128], BF16)
        i30k = sbt(top, "i30k", [128, 128], BF16)
        onesf = sbt(top, "onesf", [128, 128], F32)

        dma("sp", "d1", lambda h: h.dma_start(out=cols[:], in_=cols_d), writes=["cols"])
        dma("sp", "d1", lambda h: h.dma_start(out=rows[:], in_=rows_d), writes=["rows"])
        dma("sp", "d1", lambda h: h.dma_start(out=identf[:], in_=ident_d), writes=["identf"])
        op("act", lambda h: h.activation(out=identb[:], in_=identf[:], func=AF.Copy), reads=["identf"], writes=["identb"])
        op("dve", lambda h: h.tensor_scalar(out=i30k[:], in0=identf[:], scalar1=MASKV, scalar2=None, op0=ALU.mult), reads=["identf"], writes=["i30k"])
        op("dve", lambda h: h.memset(onesf[:], 1.0 / 512.0), writes=["onesf"])
        ncols = sbt(top, "ncols", [128, 16], F32)
        op("dve", lambda h: h.tensor_scalar(out=ncols[:], in0=cols[:, C_BG:C_BG + 16], scalar1=-1.0, scalar2=None, op0=ALU.mult),
           reads=["cols"], writes=["ncols"])

        with contextlib.ExitStack() as p01:
            ksT = sbt(p01, "ksT", [128, S], BF16)
            kwT = sbt(p01, "kwT", [128, S], BF16)
            vsw = sbt(p01, "vsw", [128, NT_ALL, 4, 65], BF16)
            kct = sbt(p01, "kct", [128, 512], BF16)
            vct = sbt(p01, "vct", [128, 4, 2, 65], BF16)
            op("dve", lambda h: h.memset(vsw[:, :, :, 64:65], 1.0), writes=["vsw1"])
            op("dve", lambda h: h.memset(vct[:], 0.0), writes=["vct"])
            op("dve", lambda h: h.memset(vct[:, :, :, 64:65], 1.0), writes=["vct"])
            op("dve", lambda h: h.memset(kct[:], 0.0), writes=["kct"])

            with contextlib.ExitStack() as p0:
                wkv = sbt(p0, "wkv", [128, 8, 768], BF16)
                kcT = sbt(p0, "kcT", [128, S], BF16)
                vcT = sbt(p0, "vcT", [128, S], BF16)
                w1 = [sbt(p0, "w1k", [128, 32, 256], BF16), sbt(p0, "w1v", [128, 32, 256], BF16)]
                w2ks = sbt(p0, "w2ks", [128, 2, 2, 128], BF16)
                w2vs = sbt(p0, "w2vs", [128, 2, 64], BF16)
                pos = [sbt(p0, "posk", [128, 32], BF16), sbt(p0, "posv", [128, 32], BF16)]
                xc = [sbt(p0, "xc0", [128, 8, 512], BF16), sbt(p0, "xc1", [128, 8, 512], BF16)]
                h1T = sbt(p0, "h1T", [128, 2, 512], BF16)
                cvec = sbt(p0, "cvec", [128, 2], F32)
                zt0 = sbt(p0, "zt0", [128, 512], F32)
                et0 = sbt(p0, "et0", [128, 512], F32)

                dma("pool", "d0", lambda h: h.dma_start(out=wkv[:], in_=w_kv), writes=["wkv"])
                for j in range(2):
                    dma("pool", "d0", lambda h, j=j: h.dma_start(out=xc[j][:], in_=xT[:, :, j * 512:(j + 1) * 512]),
                        writes=["xc%d" % j])
                for t, (wd, pd) in enumerate([(w1k, posk), (w1v, posv)]):
                    for half in range(2):
                        dma("pool", "d0", lambda h, t=t, wd=wd, half=half: h.dma_start(
                            out=w1[t][64 * half:64 * half + 64, :, :], in_=wd), writes=["w1_%d" % t])
                        dma("pool", "d0", lambda h, t=t, pd=pd, half=half: h.dma_start(
                            out=pos[t][64 * half:64 * half + 64, :], in_=pd), writes=["pos%d" % t])
                dma("pool", "d0", lambda h: h.dma_start(out=w2ks[:], in_=w2k), writes=["w2ks"])
                dma("pool", "d0", lambda h: h.dma_start(out=w2vs[:], in_=w2v), writes=["w2vs"])

                kdst = [kcT, vcT, ksT, kwT]
                for j in range(16):
                    xj = xc[j % 2]
                    xn = "xc%d" % (j % 2)
                    for cb in range(4):
                        bk = banks[cb % 2]
                        bn = "bank%d" % (cb % 2)
                        for k in range(8):
                            op("pe", lambda h, bk=bk, cb=cb, k=k, xj=xj: h.matmul(
                                bk[:], lhsT=wkv[:, k, cb * 128:(cb + 1) * 128], rhs=xj[:, k, :],
                                start=(k == 0), stop=(k == 7)), reads=["wkv", xn], writes=[bn])
                        op("act", lambda h, bk=bk, cb=cb, j=j: h.activation(
                            out=kdst[cb][:, j * 512:(j + 1) * 512], in_=bk[:], func=AF.Identity,
                            bias=cols[:, C_BKV + cb:C_BKV + cb + 1]), reads=[bn, "cols"], writes=["kd%d" % cb])
                    for tt in range(4):
                        bk = banks[2 + tt % 2]
                        bn = "bank%d" % (2 + tt % 2)
                        for k in range(8):
                            op("pe", lambda h, bk=bk, tt=tt, k=k, xj=xj: h.matmul(
                                bk[:, 0:256], lhsT=xj[:, k, tt * 128:(tt + 1) * 128], rhs=wkv[:, k, 512:768],
                                start=(k == 0), stop=(k == 7)), reads=["wkv", xn], writes=[bn])
                        op("dve", lambda h, bk=bk, tt=tt, j=j: h.tensor_tensor(
                            out=vsw[:, j * 4 + tt, :, 0:64],
                            in0=bk[:, 0:256].rearrange("p (a b) -> p a b", a=4),
                            in1=rows[:, R_BV:R_BV + 256].rearrange("p (a b) -> p a b", a=4), op=ALU.add),
                            reads=[bn, "rows"], writes=["vsw"])
                    if j + 2 < 16:
                        dma("pool", "d0", lambda h, j=j: h.dma_start(
                            out=xc[j % 2][:], in_=xT[:, :, (j + 2) * 512:(j + 3) * 512]), writes=[xn])

                for t in range(2):
                    src = kcT if t == 0 else vcT
                    srcn = "kd%d" % t
                    for g in range(2):
                        pr = slice(64 * g, 64 * g + 64)
                        for hc in range(2):
                            bc = banks[4]
                            for l in range(32):
                                op("pe", lambda h, l=l, hc=hc, t=t, pr=pr, bc=bc: h.matmul(
                                    bc[:, 0:1], lhsT=w1[t][pr, l, hc * 128:(hc + 1) * 128], rhs=pos[t][pr, l:l + 1],
                                    start=(l == 0), stop=(l == 31)), reads=["w1_%d" % t, "pos%d" % t], writes=["bank4"])
                            op("dve", lambda h, hc=hc, bc=bc: h.tensor_copy(out=cvec[:, hc:hc + 1], in_=bc[:, 0:1]),
                               reads=["bank4"], writes=["cvec"])
                            bh = banks[5 + hc]
                            bhn = "bank%d" % (5 + hc)
                            for l in range(32):
                                op("pe", lambda h, l=l, hc=hc, t=t, pr=pr, bh=bh, src=src: h.matmul(
                                    bh[:, 0:511], lhsT=w1[t][pr, l, hc * 128:(hc + 1) * 128],
                                    rhs=src[pr, :].rearrange("p (n s) -> p s n", s=16)[:, l % 16, (l // 16):(l // 16) + 511],
                                    start=(l == 0), stop=(l == 31)), reads=["w1_%d" % t, srcn], writes=[bhn])
                            op("dve", lambda h, hc=hc, bh=bh: h.tensor_scalar(
                                out=zt0[:, 0:511], in0=bh[:, 0:511], scalar1=cvec[:, hc:hc + 1], scalar2=None, op0=ALU.add),
                                reads=[bhn, "cvec"], writes=["zt0"])
                            op("act", lambda h: h.activation(out=et0[:, 0:511], in_=zt0[:, 0:511], func=AF.Exp, scale=-1.0),
                               reads=["zt0"], writes=["et0"])
                            op("dve", lambda h: h.tensor_scalar(out=et0[:, 0:511], in0=et0[:, 0:511], scalar1=1.0, scalar2=None, op0=ALU.add),
                               reads=["et0"], writes=["et0"])
                            op("dve", lambda h: h.reciprocal(out=et0[:, 0:511], in_=et0[:, 0:511]), reads=["et0"], writes=["et0"])
                            op("dve", lambda h, hc=hc: h.tensor_tensor(out=h1T[:, hc, 0:511], in0=zt0[:, 0:511], in1=et0[:, 0:511], op=ALU.mult),
                               reads=["zt0", "et0"], writes=["h1T"])
                        if t == 0:
                            bo = banks[7]
                            for hc in range(2):
                                op("pe", lambda h, hc=hc, g=g, bo=bo: h.matmul(
                                    bo[:, 0:511], lhsT=w2ks[:, g, hc, :], rhs=h1T[:, hc, 0:511],
                                    start=(hc == 0), stop=(hc == 1)), reads=["w2ks", "h1T"], writes=["bank7"])
                            op("act", lambda h, pr=pr, bo=bo: h.activation(
                                out=kct[pr, 0:511], in_=bo[pr, 0:511], func=AF.Copy), reads=["bank7"], writes=["kct"])
                        else:
                            for nt in range(4):
                                nn = 128 if nt < 3 else 127
                                bo = banks[7]
                                for hc in range(2):
                                    op("pe", lambda h, hc=hc, nt=nt, nn=nn, bo=bo: h.matmul(
                                        bo[0:nn, 0:64], lhsT=h1T[:, hc, nt * 128:nt * 128 + nn], rhs=w2vs[:, hc, :],
                                        start=(hc == 0), stop=(hc == 1)), reads=["w2vs", "h1T"], writes=["bank7"])
                                op("act", lambda h, nt=nt, nn=nn, g=g, bo=bo: h.activation(
                                    out=vct[0:nn, nt, g, 0:64], in_=bo[0:nn, 0:64], func=AF.Copy),
                                    reads=["bank7"], writes=["vct"])
                if debug:
                    for nm, t in (("ksT", ksT), ("kwT", kwT), ("vsw", vsw), ("kct", kct), ("vct", vct), ("kcT", kcT)):
                        dma("sp", "d5", lambda h, nm=nm, t=t: h.dma_start(out=dbg[nm], in_=t[:]),
                            reads=["kd0", "kd1", "kd2", "kd3", "vsw", "vsw1", "kct", "vct"], writes=["dbg" + nm])
            P.barrier()

            with contextlib.ExitStack() as p1:
                wown = sbt(p1, "wown", [128, 8, 1568], BF16)
                diag = sbt(p1, "diag", [128, 4, 31, 128], BF16)
                b30k = sbt(p1, "b30k", [128, S], BF16)
                mov = sbt(p1, "mov", [128, 4, 128], BF16)
                wmask = sbt(p1, "wmask", [128, 6, 128], BF16)
                dmask = sbt(p1, "dmask", [128, 2, 128], BF16)
                cmask = sbt(p1, "cmask", [128, 8, 2, 128], BF16)
                xt = [sbt(p1, "xt0", [128, 8, 160], BF16), sbt(p1, "xt1", [128, 8, 160], BF16)]
                bon = [sbt(p1, "bon0", [128, 128], F32), sbt(p1, "bon1", [128, 128], F32)]
                QTz4 = [[sbt(p1, "QTz%d_%d" % (sp_, g), [128, 4, 128], BF16) for g in range(2)] for sp_ in range(2)]
                gsigs = [sbt(p1, "gsig0", [128, 32], F32), sbt(p1, "gsig1", [128, 32], F32)]
                sig = [sbt(p1, "sig0", [128, 160], F32), sbt(p1, "sig1", [128, 160], F32)]
                vT = sbt(p1, "vT", [128, 4, 160], BF16)
                yconv = sbt(p1, "yconv", [128, 4, 128], F32)
                ysq = sbt(p1, "ysq", [128, 4, 128], F32)
                mean_sb = sbt(p1, "mean_sb", [128, 128], F32)
                var_sb = sbt(p1, "var_sb", [128, 128], F32)
                rstd = sbt(p1, "rstd", [128, 128], F32)
                mr = sbt(p1, "mr", [128, 128], F32)
                tnorm = [sbt(p1, "tn0", [128, 128], F32), sbt(p1, "tn1", [128, 128], F32)]
                etn = [sbt(p1, "etn0", [128, 128], F32), sbt(p1, "etn1", [128, 128], F32)]
                yT = [sbt(p1, "yT0", [128, 8, 128], BF16), sbt(p1, "yT1", [128, 8, 128], BF16)]
                PcT = [sbt(p1, "PcT0", [128, 4, 512], BF16), sbt(p1, "PcT1", [128, 4, 512], BF16)]
                PT = [sbt(p1, "PT%d" % k, [128, 512], BF16) for k in range(4)]
                Osb = [sbt(p1, "Osb0", [65, 512], F32), sbt(p1, "Osb1", [65, 512], F32)]
                rd = sbt(p1, "rd", [128, 4], F32)
                rdc = [sbt(p1, "rdc0", [128, 4], F32), sbt(p1, "rdc1", [128, 4], F32)]
                wgt = sbt(p1, "wgt", [128, 4], F32)
                imp = sbt(p1, "imp", [128, 128], F32)
                score = sbt(p1, "score", [128, 128], F32)
                m8a = sbt(p1, "m8a", [128, 8], F32)
                m8b = sbt(p1, "m8b", [128, 8], F32)
                thr = sbt(p1, "thr", [128, 1], F32)
                negs = [sbt(p1, "neg0", [128, 128], BF16), sbt(p1, "neg1", [128, 128], BF16)]
                negT = [sbt(p1, "negT0", [128, 128], BF16), sbt(p1, "negT1", [128, 128], BF16)]
                yacc = sbt(p1, "yacc", [128, 8, 64], F32)
                ytmp = sbt(p1, "ytmp", [128, 4, 64], F32)
                ynb = sbt(p1, "ynb", [128, 512], BF16)

                dma("pool", "d0", lambda h: h.dma_start(out=wown[:], in_=w_own), writes=["wown"])
                dma("pool", "d0", lambda h: h.dma_start(out=mov[:], in_=mov_d), writes=["mov"])
                dma("pool", "d0", lambda h: h.dma_start(out=wmask[:], in_=wmask_d), writes=["wmask"])
                dma("pool", "d0", lambda h: h.dma_start(out=dmask[:], in_=dmask_d), writes=["dmask"])
                dma("pool", "d0", lambda h: h.dma_start(out=cmask[:], in_=cmask_d), writes=["cmask"])
                op("pool", lambda h: h.memset(b30k[:], MASKV), writes=["b30k"])
                op("pool", lambda h: h.affine_select(out=b30k[:], in_=b30k[:], pattern=[[1, S]], compare_op=ALU.is_ge,
                                                     fill=0.0, base=0, channel_multiplier=-64),
                   reads=["b30k"], writes=["b30k"])
                op("pool", lambda h: h.affine_select(out=b30k[:], in_=b30k[:], pattern=[[-1, S]], compare_op=ALU.is_ge,
                                                     fill=0.0, base=63, channel_multiplier=64),
                   reads=["b30k"], writes=["b30k"])
                for cc in range(4):
                    for j in range(31):
                        op("dve", lambda h, cc=cc, j=j: h.tensor_scalar(
                            out=diag[:, cc, j, :], in0=identf[:], scalar1=cols[:, C_DW + cc * 31 + j:C_DW + cc * 31 + j + 1],
                            scalar2=None, op0=ALU.mult), reads=["identf", "cols"], writes=["diag"])

                def load_x(i):
                    dma("pool", "d3", lambda h, i=i: h.dma_start(out=xt[i % 2][:], in_=xTo[i]), writes=["xt%d" % (i % 2)])

                def load_bon(i):
                    dma("sp", "d1", lambda h, i=i: h.dma_start(out=bon[i % 2][:], in_=bonus_d[i]), writes=["bon%d" % (i % 2)])

                for g in range(2):
                    for sp_ in range(2):
                        op("dve", lambda h: h.memset(QTz4[sp_][g][:], 0.0), writes=["QTz%d_%d" % (sp_, g)])
                load_x(0)
                load_x(1)
                load_bon(0)
                sctr = [0]
                octr = [0]

                def mk_unit(lhsK, kn, g, masks, vlhs, vn, ob, obn, first, last, pdst=None, pdn=None, qpar=0):
                    u = sctr[0]
                    sctr[0] += 1
                    sb_ = banks[(0, 1, 6)[u % 3]]
                    sbn = "bank%d" % ((0, 1, 6)[u % 3])
                    if pdst is None:
                        pt = PT[u % 4][:]
                        ptn = "PT%d" % (u % 4)
                    else:
                        pt, ptn = pdst, pdn
                    pr = slice(64 * g, 64 * g + 64)
                    nm = len(masks)

                    def s_():
                        op("pe", lambda h: h.matmul(sb_[:], lhsT=lhsK, rhs=QTz4[qpar][g][:].rearrange("p a b -> p (a b)"),
                                                    start=True, stop=(nm == 0)), reads=[kn, "QTz%d_%d" % (qpar, g)], writes=[sbn])
                        for mi, (ml, mr_, mrn) in enumerate(masks):
                            op("pe", lambda h: h.matmul(
                                sb_[:].rearrange("p (a b) -> p a b", a=4), lhsT=ml,
                                rhs=mr_.unsqueeze(1).broadcast_to([128, 4, 128]), start=False, stop=(mi == nm - 1)),
                                reads=[mrn, "b30k", "i30k"], writes=[sbn])

                    def e_():
                        op("act", lambda h: h.activation(out=pt, in_=sb_[:], func=AF.Exp, scale=0.125),
                           reads=[sbn], writes=[ptn])

                    def pv_():
                        op("pe", lambda h: h.matmul(ob[0:65, :], lhsT=vlhs, rhs=pt, start=first, stop=last),
                           reads=[vn, ptn], writes=[obn])
                    return {"s": s_, "e": e_, "pv": pv_, "after": []}

                def run_units(units):
                    pend = []
                    n = len(units)
                    for k in range(min(2, n)):
                        units[k]["s"]()
                    for u in range(n):
                        if u + 2 < n:
                            units[u + 2]["s"]()
                        units[u]["e"]()
                        units[u]["pv"]()
                        nxt = []
                        for cnt, f in pend:
                            if cnt <= 1:
                                f()
                            else:
                                nxt.append((cnt - 1, f))
                        pend = nxt
                        for dly, f in units[u]["after"]:
                            if dly == 0:
                                f()
                            else:
                                pend.append((dly, f))
                    for _, f in pend:
                        f()

                def fin1(ob, obn):
                    k = octr[0] % 2
                    octr[0] += 1
                    osb = Osb[k]
                    osn = "Osb%d" % k
                    op("act", lambda h: h.activation(out=osb[:], in_=ob[0:65, :], func=AF.Copy), reads=[obn], writes=[osn])
                    return osb, osn

                def fin2(osb, osn, g, br, gsig, gsn):
                    ot = banks[3]
                    otv = ot[:, 192:452].rearrange("p (a b) -> p a b", a=4)
                    for hh in range(4):
                        op("pe", lambda h: h.transpose(out=ot[:, 192 + hh * 65:192 + (hh + 1) * 65],
                                                       in_=osb[:, hh * 128:(hh + 1) * 128], identity=identf[0:65, 0:65]),
                           reads=[osn, "identf"], writes=["bank3"])
                    op("dve", lambda h: h.tensor_scalar(out=rd[:], in0=otv[:, :, 64], scalar1=1e-30, scalar2=None, op0=ALU.max),
                       reads=["bank3"], writes=["rd"])
                    op("dve", lambda h: h.reciprocal(out=rd[:], in_=rd[:]), reads=["rd"], writes=["rd"])
                    if br == 0:
                        op("dve", lambda h: h.tensor_copy(out=rdc[g][:], in_=rd[:]), reads=["rd"], writes=["rdc%d" % g])
                    gc = g * 12 + br * 4
                    op("dve", lambda h: h.tensor_tensor(out=wgt[:], in0=rd[:], in1=gsig[:, gc:gc + 4], op=ALU.mult),
                       reads=["rd", gsn], writes=["wgt"])
                    wb = wgt[:].unsqueeze(2).broadcast_to([128, 4, 64])
                    if br == 0:
                        op("dve", lambda h: h.tensor_tensor(out=yacc[:, 4 * g:4 * g + 4, :], in0=otv[:, :, 0:64], in1=wb, op=ALU.mult),
                           reads=["bank3", "wgt"], writes=["yacc"])
                    else:
                        op("dve", lambda h: h.tensor_tensor(out=ytmp[:], in0=otv[:, :, 0:64], in1=wb, op=ALU.mult),
                           reads=["bank3", "wgt"], writes=["ytmp"])
                        op("dve", lambda h: h.tensor_tensor(out=yacc[:, 4 * g:4 * g + 4, :], in0=yacc[:, 4 * g:4 * g + 4, :],
                                                            in1=ytmp[:], op=ALU.add),
                           reads=["yacc", "ytmp"], writes=["yacc"])

                def topk_chain(g, bo_, bon_n, bu, bun):
                    op("dve", lambda h: h.tensor_scalar(out=imp[:], in0=bu[:, 0:128], scalar1=rdc[g][:, 0:1], scalar2=None, op0=ALU.mult),
                       reads=[bun, "rdc%d" % g], writes=["imp"])
                    for hh in range(1, 4):
                        op("dve", lambda h: h.scalar_tensor_tensor(
                            out=imp[:], in0=bu[:, hh * 128:(hh + 1) * 128], scalar=rdc[g][:, hh:hh + 1], in1=imp[:],
                            op0=ALU.mult, op1=ALU.add), reads=[bun, "rdc%d" % g, "imp"], writes=["imp"])
                    op("dve", lambda h: h.tensor_tensor(out=score[:], in0=imp[:], in1=bo_[:], op=ALU.add),
                       reads=["imp", bon_n], writes=["score"])
                    op("dve", lambda h: h.max(out=m8a[:], in_=score[:]), reads=["score"], writes=["m8a"])
                    op("dve", lambda h: h.match_replace(out=imp[:], in_to_replace=m8a[:], in_values=score[:], imm_value=-3e30),
                       reads=["score", "m8a"], writes=["imp"])
                    op("dve", lambda h: h.max(out=m8b[:], in_=imp[:]), reads=["imp"], writes=["m8b"])
                    op("dve", lambda h: h.tensor_scalar(out=thr[:], in0=m8b[:, 7:8], scalar1=-1e30, scalar2=None, op0=ALU.max),
                       reads=["m8b"], writes=["thr"])
                    op("dve", lambda h: h.tensor_scalar(out=negs[g][:], in0=score[:], scalar1=thr[:, 0:1], scalar2=1.0,
                                                        op0=ALU.is_ge, op1=ALU.subtract),
                       reads=["score", "thr"], writes=["neg%d" % g])

                tail = [None]

                def stage1(j):
                    x_ = xt[j % 2]
                    xn = "xt%d" % (j % 2)
                    gsig = gsigs[j % 2]
                    gsn = "gsig%d" % (j % 2)
                    bq = banks[2]
                    for hh in range(4):
                        for k in range(8):
                            op("pe", lambda h: h.matmul(
                                bq[:, hh * 128:(hh + 1) * 128], lhsT=wown[:, k, hh * 128:(hh + 1) * 128], rhs=x_[:, k, 32:160],
                                start=(k == 0), stop=(k == 7)), reads=["wown", xn], writes=["bank2"])
                    bg = banks[3]
                    for k in range(8):
                        op("pe", lambda h: h.matmul(bg[:, 0:32], lhsT=x_[:, k, 32:160], rhs=wown[:, k, 1536:1568],
                                                    start=(k == 0), stop=(k == 7)), reads=["wown", xn], writes=["bank3"])
                    for g in range(2):
                        pr = slice(64 * g, 64 * g + 64)
                        op("dve", lambda h: h.tensor_tensor(
                            out=QTz4[j % 2][g][pr, :, :], in0=bq[pr, :].rearrange("p (a b) -> p a b", a=4),
                            in1=cols[pr, C_BQ:C_BQ + 4].unsqueeze(2).broadcast_to([64, 4, 128]), op=ALU.add),
                            reads=["bank2", "cols"], writes=["QTz%d_%d" % (j % 2, g)])
                    op("dve", lambda h: h.tensor_tensor(out=gsig[:], in0=bg[:, 0:32], in1=rows[:, R_BGT:R_BGT + 32], op=ALU.add),
                       reads=["bank3", "rows"], writes=[gsn])
                    op("act", lambda h: h.activation(out=gsig[:], in_=gsig[:], func=AF.Exp, scale=-1.0), reads=[gsn], writes=[gsn])
                    op("dve", lambda h: h.tensor_scalar(out=gsig[:], in0=gsig[:], scalar1=1.0, scalar2=None, op0=ALU.add),
                       reads=[gsn], writes=[gsn])
                    op("dve", lambda h: h.reciprocal(out=gsig[:], in_=gsig[:]), reads=[gsn], writes=[gsn])

                def conv_proj(j, cc):
                    x_ = xt[j % 2]
                    xn = "xt%d" % (j % 2)
                    ba = banks[2]
                    bgp = banks[3]
                    for k in range(8):
                        op("pe", lambda h: h.matmul(
                            ba[:, 0:160], lhsT=wown[:, k, 512 + cc * 128:512 + (cc + 1) * 128], rhs=x_[:, k, :],
                            start=(k == 0), stop=(k == 7)), reads=["wown", xn], writes=["bank2"])
                    for k in range(8):
                        op("pe", lambda h: h.matmul(
                            bgp[:, 0:160], lhsT=wown[:, k, 1024 + cc * 128:1024 + (cc + 1) * 128], rhs=x_[:, k, :],
                            start=(k == 0), stop=(k == 7)), reads=["wown", xn], writes=["bank3"])
                    sg = sig[cc % 2]
                    sgn = "sig%d" % (cc % 2)
                    op("act", lambda h: h.activation(
                        out=sg[:], in_=bgp[:, 0:160], func=AF.Exp, bias=ncols[:, cc:cc + 1], scale=-1.0),
                        reads=["bank3", "ncols"], writes=[sgn])
                    op("dve", lambda h: h.tensor_scalar(out=sg[:], in0=sg[:], scalar1=1.0, scalar2=None, op0=ALU.add),
                       reads=[sgn], writes=[sgn])
                    op("dve", lambda h: h.reciprocal(out=sg[:], in_=sg[:]), reads=[sgn], writes=[sgn])
                    op("dve", lambda h: h.scalar_tensor_tensor(
                        out=vT[:, cc, :], in0=ba[:, 0:160], scalar=cols[:, C_BA + cc:C_BA + cc + 1], in1=sg[:],
                        op0=ALU.add, op1=ALU.mult), reads=["bank2", "cols", sgn], writes=["vT"])
                    if j == 0:
                        op("dve", lambda h: h.tensor_scalar(
                            out=vT[:, cc, 0:32], in0=vT[:, cc, 0:32], scalar1=cols[:, C_HALO:C_HALO + 1], scalar2=None,
                            op0=ALU.mult), reads=["vT", "cols"], writes=["vT"])

                def conv_taps(cc):
                    by = banks[2 + cc % 2]
                    byn = "bank%d" % (2 + cc % 2)
                    for jj in range(31):
                        op("pe", lambda h: h.matmul(
                            by[:, 0:128], lhsT=diag[:, cc, jj, :], rhs=vT[:, cc, 2 + jj:2 + jj + 128],
                            start=(jj == 0), stop=(jj == 30)), reads=["diag", "vT"], writes=[byn])
                    op("act", lambda h: h.activation(
                        out=yconv[:, cc, :], in_=by[:, 0:128], func=AF.Identity, bias=cols[:, C_DWB + cc:C_DWB + cc + 1]),
                        reads=[byn, "cols"], writes=["yconv"])
                    op("act", lambda h: h.activation(
                        out=ysq[:, cc, :], in_=by[:, 0:128], func=AF.Square, bias=cols[:, C_DWB + cc:C_DWB + cc + 1]),
                        reads=[byn, "cols"], writes=["ysq"])

                def conv_stats():
                    bm = banks[2]
                    bs = banks[3]
                    for cc in range(4):
                        op("pe", lambda h: h.matmul(bm[:, 0:128], lhsT=onesf[:], rhs=yconv[:, cc, :],
                                                    start=(cc == 0), stop=(cc == 3)), reads=["onesf", "yconv"], writes=["bank2"])
                    for cc in range(4):
                        op("pe", lambda h: h.matmul(bs[:, 0:128], lhsT=onesf[:], rhs=ysq[:, cc, :],
                                                    start=(cc == 0), stop=(cc == 3)), reads=["onesf", "ysq"], writes=["bank3"])
                    op("act", lambda h: h.activation(out=mean_sb[:], in_=bm[:, 0:128], func=AF.Copy), reads=["bank2"], writes=["mean_sb"])
                    op("dve", lambda h: h.tensor_tensor(out=var_sb[:], in0=mean_sb[:], in1=mean_sb[:], op=ALU.mult),
                       reads=["mean_sb"], writes=["var_sb"])
                    op("dve", lambda h: h.tensor_tensor(out=var_sb[:], in0=bs[:, 0:128], in1=var_sb[:], op=ALU.subtract),
                       reads=["bank3", "var_sb"], writes=["var_sb"])
                    op("dve", lambda h: h.tensor_scalar(out=var_sb[:], in0=var_sb[:], scalar1=0.0, scalar2=EPS, op0=ALU.max, op1=ALU.add),
                       reads=["var_sb"], writes=["var_sb"])

                def conv_ln_b():
                    op("act", lambda h: h.activation(out=rstd[:], in_=var_sb[:], func=AF.Ln), reads=["var_sb"], writes=["rstd"])
                    op("act", lambda h: h.activation(out=rstd[:], in_=rstd[:], func=AF.Exp, scale=-0.5), reads=["rstd"], writes=["rstd"])
                    op("dve", lambda h: h.tensor_tensor(out=mr[:], in0=mean_sb[:], in1=rstd[:], op=ALU.mult),
                       reads=["mean_sb", "rstd"], writes=["mr"])

                def conv_ln_c(cc):
                    tn = tnorm[cc % 2]
                    tnn = "tn%d" % (cc % 2)
                    op("dve", lambda h: h.tensor_tensor(out=tn[:], in0=yconv[:, cc, :], in1=rstd[:], op=ALU.mult),
                       reads=["yconv", "rstd"], writes=[tnn])
                    op("dve", lambda h: h.tensor_tensor(out=tn[:], in0=tn[:], in1=mr[:], op=ALU.subtract),
                       reads=[tnn, "mr"], writes=[tnn])

                def conv_ln_d(j, cc):
                    yt = yT[j % 2]
                    ytn = "yT%d" % (j % 2)
                    tn = tnorm[cc % 2]
                    tnn = "tn%d" % (cc % 2)
                    et = etn[cc % 2]
                    etnn = "etn%d" % (cc % 2)
                    op("act", lambda h: h.activation(
                        out=et[:], in_=tn[:], func=AF.Exp, bias=ncols[:, 12 + cc:13 + cc],
                        scale=ncols[:, 8 + cc:9 + cc]), reads=[tnn, "ncols"], writes=[etnn])
                    op("dve", lambda h: h.tensor_scalar(
                        out=tn[:], in0=tn[:], scalar1=cols[:, C_LNG + cc:C_LNG + cc + 1], scalar2=cols[:, C_LNB + cc:C_LNB + cc + 1],
                        op0=ALU.mult, op1=ALU.add), reads=[tnn, "cols"], writes=[tnn])
                    op("dve", lambda h: h.tensor_scalar(out=et[:], in0=et[:], scalar1=1.0, scalar2=None, op0=ALU.add),
                       reads=[etnn], writes=[etnn])
                    op("dve", lambda h: h.reciprocal(out=et[:], in_=et[:]), reads=[etnn], writes=[etnn])
                    op("dve", lambda h: h.tensor_tensor(out=yt[:, cc, :], in0=tn[:], in1=et[:], op=ALU.mult),
                       reads=[tnn, etnn], writes=[ytn])

                def hoisted_pieces(j):
                    x_ = xt[j % 2]
                    xn = "xt%d" % (j % 2)
                    gsig = gsigs[j % 2]
                    gsn = "gsig%d" % (j % 2)
                    yt = yT[j % 2]
                    ytn = "yT%d" % (j % 2)
                    abank = [(banks[2], "bank2"), (banks[7], "bank7")]

                    def s1_pe():
                        bq = banks[2]
                        for hh in range(4):
                            for k in range(8):
                                op("pe", lambda h: h.matmul(
                                    bq[:, hh * 128:(hh + 1) * 128], lhsT=wown[:, k, hh * 128:(hh + 1) * 128], rhs=x_[:, k, 32:160],
                                    start=(k == 0), stop=(k == 7)), reads=["wown", xn], writes=["bank2"])
                        bg = banks[3]
                        for k in range(8):
                            op("pe", lambda h: h.matmul(bg[:, 0:32], lhsT=x_[:, k, 32:160], rhs=wown[:, k, 1536:1568],
                                                        start=(k == 0), stop=(k == 7)), reads=["wown", xn], writes=["bank3"])

                    def s1_dve():
                        bq = banks[2]
                        bg = banks[3]
                        for g in range(2):
                            pr = slice(64 * g, 64 * g + 64)
                            op("dve", lambda h: h.tensor_tensor(
                                out=QTz4[j % 2][g][pr, :, :], in0=bq[pr, :].rearrange("p (a b) -> p a b", a=4),
                                in1=cols[pr, C_BQ:C_BQ + 4].unsqueeze(2).broadcast_to([64, 4, 128]), op=ALU.add),
                                reads=["bank2", "cols"], writes=["QTz%d_%d" % (j % 2, g)])
                        op("dve", lambda h: h.tensor_tensor(out=gsig[:], in0=bg[:, 0:32], in1=rows[:, R_BGT:R_BGT + 32], op=ALU.add),
                           reads=["bank3", "rows"], writes=[gsn])

                    def gate_act():
                        op("act", lambda h: h.activation(out=gsig[:], in_=gsig[:], func=AF.Exp, scale=-1.0), reads=[gsn], writes=[gsn])

                    def gate_fin():
                        op("dve", lambda h: h.tensor_scalar(out=gsig[:], in0=gsig[:], scalar1=1.0, scalar2=None, op0=ALU.add),
                           reads=[gsn], writes=[gsn])
                        op("dve", lambda h: h.reciprocal(out=gsig[:], in_=gsig[:]), reads=[gsn], writes=[gsn])

                    def cp_pe(cc):
                        ba, ban = abank[cc % 2]
                        bgp = banks[3]
                        for k in range(8):
                            op("pe", lambda h: h.matmul(
                                ba[:, 0:160], lhsT=wown[:, k, 512 + cc * 128:512 + (cc + 1) * 128], rhs=x_[:, k, :],
                                start=(k == 0), stop=(k == 7)), reads=["wown", xn], writes=[ban])
                        for k in range(8):
                            op("pe", lambda h: h.matmul(
                                bgp[:, 0:160], lhsT=wown[:, k, 1024 + cc * 128:1024 + (cc + 1) * 128], rhs=x_[:, k, :],
                                start=(k == 0), stop=(k == 7)), reads=["wown", xn], writes=["bank3"])

                    def cp_act(cc):
                        sg = sig[cc % 2]
                        op("act", lambda h: h.activation(
                            out=sg[:], in_=banks[3][:, 0:160], func=AF.Exp, bias=ncols[:, cc:cc + 1], scale=-1.0),
                            reads=["bank3", "ncols"], writes=["sig%d" % (cc % 2)])

                    def cp_dve(cc):
                        ba, ban = abank[cc % 2]
                        sg = sig[cc % 2]
                        sgn = "sig%d" % (cc % 2)
                        op("dve", lambda h: h.tensor_scalar(out=sg[:], in0=sg[:], scalar1=1.0, scalar2=None, op0=ALU.add),
                           reads=[sgn], writes=[sgn])
                        op("dve", lambda h: h.reciprocal(out=sg[:], in_=sg[:]), reads=[sgn], writes=[sgn])
                        op("dve", lambda h: h.scalar_tensor_tensor(
                            out=vT[:, cc, :], in0=ba[:, 0:160], scalar=cols[:, C_BA + cc:C_BA + cc + 1], in1=sg[:],
                            op0=ALU.add, op1=ALU.mult), reads=[ban, "cols", sgn], writes=["vT"])
                        if j == 0:
                            op("dve", lambda h: h.tensor_scalar(
                                out=vT[:, cc, 0:32], in0=vT[:, cc, 0:32], scalar1=cols[:, C_HALO:C_HALO + 1], scalar2=None,
                                op0=ALU.mult), reads=["vT", "cols"], writes=["vT"])

                    def tp_pe(cc):
                        by = banks[2 + cc % 2]
                        byn = "bank%d" % (2 + cc % 2)
                        for jj in range(31):
                            op("pe", lambda h: h.matmul(
                                by[:, 0:128], lhsT=diag[:, cc, jj, :], rhs=vT[:, cc, 2 + jj:2 + jj + 128],
                                start=(jj == 0), stop=(jj == 30)), reads=["diag", "vT"], writes=[byn])

                    def tp_dve(cc):
                        by = banks[2 + cc % 2]
                        byn = "bank%d" % (2 + cc % 2)
                        op("dve", lambda h: h.tensor_scalar(
                            out=yconv[:, cc, :], in0=by[:, 0:128], scalar1=cols[:, C_DWB + cc:C_DWB + cc + 1], scalar2=None, op0=ALU.add),
                            reads=[byn, "cols"], writes=["yconv"])
                        op("dve", lambda h: h.tensor_tensor(out=ysq[:, cc, :], in0=yconv[:, cc, :], in1=yconv[:, cc, :], op=ALU.mult),
                           reads=["yconv"], writes=["ysq"])

                    def st_pe():
                        for cc in range(4):
                            op("pe", lambda h: h.matmul(banks[2][:, 0:128], lhsT=onesf[:], rhs=yconv[:, cc, :],
                                                        start=(cc == 0), stop=(cc == 3)), reads=["onesf", "yconv"], writes=["bank2"])
                        for cc in range(4):
                            op("pe", lambda h: h.matmul(banks[3][:, 0:128], lhsT=onesf[:], rhs=ysq[:, cc, :],
                                                        start=(cc == 0), stop=(cc == 3)), reads=["onesf", "ysq"], writes=["bank3"])

                    def st_dve():
                        op("dve", lambda h: h.tensor_copy(out=mean_sb[:], in_=banks[2][:, 0:128]), reads=["bank2"], writes=["mean_sb"])
                        op("dve", lambda h: h.tensor_tensor(out=var_sb[:], in0=mean_sb[:], in1=mean_sb[:], op=ALU.mult),
                           reads=["mean_sb"], writes=["var_sb"])
                        op("dve", lambda h: h.tensor_tensor(out=var_sb[:], in0=banks[3][:, 0:128], in1=var_sb[:], op=ALU.subtract),
                           reads=["bank3", "var_sb"], writes=["var_sb"])
                        op("dve", lambda h: h.tensor_scalar(out=var_sb[:], in0=var_sb[:], scalar1=0.0, scalar2=EPS, op0=ALU.max, op1=ALU.add),
                           reads=["var_sb"], writes=["var_sb"])

                    def rs_act():
                        op("act", lambda h: h.activation(out=rstd[:], in_=var_sb[:], func=AF.Ln), reads=["var_sb"], writes=["rstd"])
                        op("act", lambda h: h.activation(out=rstd[:], in_=rstd[:], func=AF.Exp, scale=-0.5), reads=["rstd"], writes=["rstd"])

                    def mr_dve():
                        op("dve", lambda h: h.tensor_tensor(out=mr[:], in0=mean_sb[:], in1=rstd[:], op=ALU.mult),
                           reads=["mean_sb", "rstd"], writes=["mr"])

                    def sl_act(cc):
                        tn = tnorm[cc % 2]
                        et = etn[cc % 2]
                        op("act", lambda h: h.activation(
                            out=et[:], in_=tn[:], func=AF.Exp, bias=ncols[:, 12 + cc:13 + cc],
                            scale=ncols[:, 8 + cc:9 + cc]), reads=["tn%d" % (cc % 2), "ncols"], writes=["etn%d" % (cc % 2)])

                    def sl_dve(cc):
                        tn = tnorm[cc % 2]
                        tnn = "tn%d" % (cc % 2)
                        et = etn[cc % 2]
                        etnn = "etn%d" % (cc % 2)
                        op("dve", lambda h: h.tensor_scalar(
                            out=tn[:], in0=tn[:], scalar1=cols[:, C_LNG + cc:C_LNG + cc + 1], scalar2=cols[:, C_LNB + cc:C_LNB + cc + 1],
                            op0=ALU.mult, op1=ALU.add), reads=[tnn, "cols"], writes=[tnn])
                        op("dve", lambda h: h.tensor_scalar(out=et[:], in0=et[:], scalar1=1.0, scalar2=None, op0=ALU.add),
                           reads=[etnn], writes=[etnn])
                        op("dve", lambda h: h.reciprocal(out=et[:], in_=et[:]), reads=[etnn], writes=[etnn])
                        op("dve", lambda h: h.tensor_tensor(out=yt[:, cc, :], in0=tn[:], in1=et[:], op=ALU.mult),
                           reads=[tnn, etnn], writes=[ytn])

                    seq = [
                        [s1_pe],
                        [s1_dve, lambda: cp_pe(0)],
                        [gate_act, lambda: cp_act(0)],
                        [gate_fin, lambda: cp_dve(0), lambda: cp_pe(1)],
                        [lambda: cp_act(1)],
                        [lambda: cp_dve(1), lambda: cp_pe(2)],
                        [lambda: cp_act(2)],
                        [lambda: cp_dve(2), lambda: cp_pe(3)],
                        [lambda: cp_act(3)],
                        [lambda: cp_dve(3), lambda: tp_pe(0)],
                        [lambda: tp_dve(0), lambda: tp_pe(1)],
                        [lambda: tp_dve(1), lambda: tp_pe(2)],
                        [lambda: tp_dve(2), lambda: tp_pe(3)],
                        [lambda: tp_dve(3)],
                        [st_pe],
                        [st_dve],
                        [rs_act],
                        [mr_dve, lambda: conv_ln_c(0)],
                        [lambda: sl_act(0), lambda: conv_ln_c(1)],
                        [lambda: sl_dve(0), lambda: sl_act(1), lambda: conv_ln_c(2)],
                        [lambda: sl_dve(1), lambda: sl_act(2), lambda: conv_ln_c(3)],
                        [lambda: sl_dve(2), lambda: sl_act(3)],
                        [lambda: sl_dve(3)],
                    ]
                    return [(lambda fs=fs: [f() for f in fs]) for fs in seq]

                for f in hoisted_pieces(0):
                    f()
                for i in range(NSTEP):
                    bo_ = bon[i % 2]
                    bon_n = "bon%d" % (i % 2)
                    yt = yT[i % 2]
                    ytn = "yT%d" % (i % 2)
                    gsig = gsigs[i % 2]
                    gsn = "gsig%d" % (i % 2)
                    qp = i % 2
                    if i + 2 < NSTEP:
                        load_x(i + 2)
                    if i + 1 < NSTEP:
                        load_bon(i + 1)
                    if i < 8:
                        dma("pool", "d6", lambda h: h.dma_start(out=wupb[:, i:i + 1, :], in_=w_up[:, i:i + 1, :]), writes=["wupb"])
                    elif i < 16:
                        j8 = i - 8
                        dma("pool", "d6", lambda h: h.dma_start(out=wdnb[:, 4 * j8:4 * j8 + 4, :], in_=w_dn[:, 4 * j8:4 * j8 + 4, :]),
                            writes=["wdnb"])

                    NTc = (16 * i + 14) // 128 + 1
                    cmp_fin = []
                    units = []
                    for g in range(2):
                        pc = PcT[g]
                        pcn = "PcT%d" % g
                        ob = banks[4 + g]
                        obn = "bank%d" % (4 + g)
                        for nt in range(NTc):
                            masks = []
                            if nt >= NTc - 2:
                                which = 1 if nt == NTc - 1 else 0
                                masks.append((i30k[:], cmask[:, i % 8, which, :], "cmask"))
                            units.append(mk_unit(kct[:, nt * 128:(nt + 1) * 128], "kct", g, masks,
                                                 vct[:, nt, g, :], "vct", ob, obn, nt == 0, nt == NTc - 1,
                                                 pdst=pc[:, nt, :], pdn=pcn, qpar=qp))
                        cmp_fin.append((ob, obn, pc, pcn))
                    if tail[0] is not None:
                        for pi, pf in enumerate(tail[0]):
                            units[min(pi, len(units) - 1)]["after"].append((0, pf))
                        tail[0] = None
                    run_units(units)
                    ubank = [(banks[7], "bank7"), (banks[2], "bank2")]
                    for g in range(2):
                        ob, obn, pc, pcn = cmp_fin[g]
                        osb, osn = fin1(ob, obn)
                        cmp_fin[g] = (osb, osn, pc, pcn)
                        bu, bun = ubank[g]
                        for hh in range(4):
                            for nt in range(NTc):
                                op("pe", lambda h: h.matmul(
                                    bu[:, hh * 128:(hh + 1) * 128], lhsT=pc[:, nt, hh * 128:(hh + 1) * 128], rhs=mov[:, nt, :],
                                    start=(nt == 0), stop=(nt == NTc - 1)), reads=[pcn, "mov"], writes=[bun])

                    def cmp_epi(g):
                        osb, osn, pc, pcn = cmp_fin[g]
                        fin2(osb, osn, g, 0, gsig, gsn)

                    def cmp_topk(g):
                        topk_chain(g, bo_, bon_n, ubank[g][0], ubank[g][1])

                    def cmp_post(g):
                        cmp_epi(g)
                        cmp_topk(g)

                    units = []
                    for g in range(2):
                        ob = banks[4 + g]
                        obn = "bank%d" % (4 + g)
                        kts = [kt for kt in range(2 * i - 4, 2 * i + 2) if kt >= 0]
                        for idx, kt in enumerate(kts):
                            sl = kt - (2 * i - 4)
                            masks = [] if sl in (2, 3) else [(i30k[:], wmask[:, sl, :], "wmask")]
                            units.append(mk_unit(kwT[:, kt * 128:(kt + 1) * 128], "kwT", g, masks,
                                                 vsw[:, kt, 2 + g, :], "vsw", ob, obn, idx == 0, idx == len(kts) - 1, qpar=qp))

                        def fw(ob=ob, obn=obn, g=g, units=units, at=len(units) - 1, gsig=gsig, gsn=gsn):
                            osb, osn = fin1(ob, obn)
                            units[at]["after"].append((2, lambda: fin2(osb, osn, g, 2, gsig, gsn)))
                        units[-1]["after"].append((0, fw))
                    nk = 2 * i + 2
                    defer_g1 = nk >= 24
                    units[0]["after"].insert(0, (0, lambda: cmp_post(0)))
                    if not defer_g1:
                        units[min(3, len(units) - 1)]["after"].insert(0, (0, lambda: cmp_post(1)))
                    else:
                        units[3]["after"].insert(0, (0, lambda: cmp_epi(1)))
                    run_units(units)

                    def neg_pe(g):
                        bt = banks[7]
                        op("pe", lambda h: h.matmul(bt[:, 256 + g * 128:256 + (g + 1) * 128], lhsT=negs[g][:], rhs=identb[:],
                                                    start=True, stop=True), reads=["neg%d" % g, "identb"], writes=["bank7"])

                    def neg_cp(g):
                        bt = banks[7]
                        op("dve", lambda h: h.tensor_copy(out=negT[g][:], in_=bt[:, 256 + g * 128:256 + (g + 1) * 128]),
                           reads=["bank7"], writes=["negT%d" % g])
                    neg_pe(0)
                    neg_cp(0)
                    if not defer_g1:
                        neg_pe(1)
                        neg_cp(1)

                    units = []
                    last_fin = {}
                    for g in range(2):
                        ob = banks[4 + g]
                        obn = "bank%d" % (4 + g)
                        for kt in range(nk):
                            masks = [(b30k[:, kt * 128:(kt + 1) * 128], negT[g][:], "negT%d" % g)]
                            if kt >= 2 * i:
                                masks.append((i30k[:], dmask[:, kt - 2 * i, :], "dmask"))
                            units.append(mk_unit(ksT[:, kt * 128:(kt + 1) * 128], "ksT", g, masks,
                                                 vsw[:, kt, g, :], "vsw", ob, obn, kt == 0, kt == nk - 1, qpar=qp))
                        if g == 0:
                            def fs(ob=ob, obn=obn, units=units, at=len(units) - 1, gsig=gsig, gsn=gsn):
                                osb, osn = fin1(ob, obn)
                                units[at]["after"].append((2, lambda: fin2(osb, osn, 0, 1, gsig, gsn)))
                            units[-1]["after"].append((0, fs))
                        else:
                            def fs1(ob=ob, obn=obn):
                                last_fin["o"] = fin1(ob, obn)
                            units[-1]["after"].append((0, fs1))
                    if i + 1 < NSTEP:
                        hp = hoisted_pieces(i + 1)
                        by_unit = {}
                        off = 14 if defer_g1 else 1
                        sp_ = 2 if len(units) >= 2 * len(hp) + off + 1 else 1
                        for pi, pf in enumerate(hp):
                            by_unit.setdefault(min(off + sp_ * pi, len(units) - 1), []).append((0, pf))
                        for at, lst in by_unit.items():
                            units[at]["after"] = lst + units[at]["after"]
                    if defer_g1:
                        units[0]["after"].insert(0, (0, lambda: cmp_topk(1)))
                        units[12]["after"].insert(0, (0, lambda: neg_pe(1)))
                        units[13]["after"].insert(0, (0, lambda: neg_cp(1)))
                    run_units(units)

                    def mk_tail(i=i, yt=yt, ytn=ytn, lf=last_fin, gsig=gsig, gsn=gsn):
                        def ta():
                            osb, osn = lf["o"]
                            fin2(osb, osn, 1, 1, gsig, gsn)

                        def tb():
                            op("dve", lambda h: h.tensor_copy(out=ynb[:], in_=yacc[:].rearrange("p a b -> p (a b)")),
                               reads=["yacc"], writes=["ynb"])

                        def tc():
                            bt = banks[7]
                            for fc in range(4):
                                op("pe", lambda h: h.matmul(bt[:, fc * 128:(fc + 1) * 128], lhsT=ynb[:, fc * 128:(fc + 1) * 128],
                                                            rhs=identb[:], start=True, stop=True),
                                   reads=["ynb", "identb"], writes=["bank7"])

                        def td():
                            bt = banks[7]
                            op("dve", lambda h: h.tensor_copy(out=yt[:, 4:8, :], in_=bt[:].rearrange("p (a b) -> p a b", a=4)),
                               reads=["bank7"], writes=[ytn])
                            dma("sp", "d2", lambda h: h.dma_start(out=ystash[i], in_=yt[:]), reads=[ytn], writes=["ystash%d" % (i % 2)])
                        return [ta, tb, tc, td]
                    tail[0] = mk_tail()
                for pf in tail[0]:
                    pf()
        P.barrier()

        with contextlib.ExitStack() as p2:
            wout = sbt(p2, "wout", [128, 8, D], BF16)
            wpg = sbt(p2, "wpg", [128, 8, D], BF16)
            wpe = sbt(p2, "wpe", [128, 2, D], BF16)
            wup = [sbt(p2, "wup0", [128, 8, 512], BF16), sbt(p2, "wup1", [128, 8, 512], BF16)]
            wdn = [sbt(p2, "wdn0", [128, 4, 512], BF16), sbt(p2, "wdn1", [128, 4, 512], BF16)]
            yTb = [sbt(p2, "yTb0", [128, 4, 8, 128], BF16)] * 2
            pTb = [sbt(p2, "pTb0", [128, 4, 2, 128], BF16)] * 2
            xin = [sbt(p2, "xin0", [128, D], F32), sbt(p2, "xin1", [128, D], F32)]
            x1f = [sbt(p2, "x1f%d" % k, [128, 4, D], F32) for k in range(2)]
            x1b = [sbt(p2, "x1b0", [128, D], BF16), sbt(p2, "x1b1", [128, D], BF16)]
            x1T = [sbt(p2, "x1T%d" % k, [128, 8, 4, 128], BF16) for k in range(2)]
            hT = sbt(p2, "hT", [128, 32, 512], BF16)
            htmp = [sbt(p2, "htmp0", [128, 512], F32), sbt(p2, "htmp1", [128, 512], F32)]
            r1s = [sbt(p2, "r1_0", [128, D], F32), sbt(p2, "r1_1", [128, D], F32)]
            sgp = sbt(p2, "sgp", [128, 512], F32)
            stats = sbt(p2, "stats", [128, 8, 2, 6], F32)
            mv = sbt(p2, "mv", [128, 8, 2], F32)
            rs = sbt(p2, "rs", [128, 8], F32)
            obuf = [sbt(p2, "obuf0", [128, D], F32), sbt(p2, "obuf1", [128, D], F32)]

            dma("pool", "d0", lambda h: h.dma_start(out=wout[:], in_=w_out), writes=["wout"])
            dma("pool", "d0", lambda h: h.dma_start(out=wpg[:], in_=w_pg), writes=["wpg"])
            dma("pool", "d0", lambda h: h.dma_start(out=wpe[:], in_=w_pe), writes=["wpe"])

            def ln_a(src, srcn, k):
                for hf in range(2):
                    op("dve", lambda h: h.bn_stats(out=stats[:, k, hf, :], in_=src[:, hf * 512:(hf + 1) * 512]),
                       reads=[srcn], writes=["stats%d" % k])
                op("dve", lambda h: h.bn_aggr(out=mv[:, k, :], in_=stats[:, k, :, :].rearrange("p a b -> p (a b)")),
                   reads=["stats%d" % k], writes=["mv%d" % k])
                op("dve", lambda h: h.tensor_scalar(out=rs[:, k:k + 1], in0=mv[:, k, 1:2], scalar1=0.0, scalar2=EPS, op0=ALU.max, op1=ALU.add),
                   reads=["mv%d" % k], writes=["rs%d" % k])

            def ln_b(src, srcn, dst, dstn, k, rg, rb):
                op("act", lambda h: h.activation(out=rs[:, k:k + 1], in_=rs[:, k:k + 1], func=AF.Ln), reads=["rs%d" % k], writes=["rs%d" % k])
                op("act", lambda h: h.activation(out=rs[:, k:k + 1], in_=rs[:, k:k + 1], func=AF.Exp, scale=-0.5),
                   reads=["rs%d" % k], writes=["rs%d" % k])
                op("dve", lambda h: h.tensor_scalar(out=dst, in0=src[:], scalar1=mv[:, k, 0:1], scalar2=rs[:, k:k + 1],
                                                    op0=ALU.subtract, op1=ALU.mult), reads=[srcn, "mv%d" % k, "rs%d" % k], writes=[dstn])
                op("dve", lambda h: h.tensor_tensor(out=dst, in0=dst, in1=rows[:, rg:rg + D], op=ALU.mult),
                   reads=[dstn, "rows"], writes=[dstn])
                op("dve", lambda h: h.tensor_tensor(out=dst, in0=dst, in1=rows[:, rb:rb + D], op=ALU.add),
                   reads=[dstn, "rows"], writes=[dstn])

            deferred = []

            def pre_pieces(blk):
                par = blk % 2
                yb, pb, xf, xT = yTb[par], pTb[par], x1f[par], x1T[par]
                ybn, pbn, xfn, xTn = "yTb0", "pTb0", "x1f%d" % par, "x1T%d" % par

                def loads():
                    for tl in range(4):
                        ti = blk * 4 + tl
                        dma("sp", "d1", lambda h: h.dma_start(out=yb[:, tl, :, :], in_=ystash[ti]),
                            reads=["ystash0", "ystash1"], writes=[ybn + "_%d" % tl])
                        dma("pool", "d3", lambda h: h.dma_start(out=pb[:, tl, :, :], in_=pTo[ti]), writes=[pbn + "_%d" % tl])

                def outproj(tl, hf):
                    ti = blk * 4 + tl
                    xi = xin[tl % 2]
                    r1_ = r1s[tl % 2]
                    if hf == 0:
                        dma("pool", "d1", lambda h: h.dma_start(out=xi[:], in_=xo[ti]), writes=["xin%d" % (tl % 2)])
                    bk = banks[2 + hf]
                    for kc in range(8):
                        op("pe", lambda h: h.matmul(
                            bk[:], lhsT=yb[:, tl, kc, :], rhs=wout[:, kc, hf * 512:(hf + 1) * 512],
                            start=(kc == 0), stop=(kc == 7)), reads=[ybn + "_%d" % tl, "wout"], writes=["bank%d" % (2 + hf)])
                    op("dve", lambda h: h.tensor_tensor(
                        out=r1_[:, hf * 512:(hf + 1) * 512], in0=bk[:], in1=rows[:, R_BOUT + hf * 512:R_BOUT + (hf + 1) * 512],
                        op=ALU.add), reads=["bank%d" % (2 + hf), "rows"], writes=["r1_%d" % (tl % 2)])

                def lnA(tl):
                    xi = xin[tl % 2]
                    r1_ = r1s[tl % 2]
                    op("dve", lambda h: h.scalar_tensor_tensor(out=r1_[:], in0=xi[:], scalar=ALPHA, in1=r1_[:],
                                                               op0=ALU.mult, op1=ALU.add),
                       reads=["xin%d" % (tl % 2), "r1_%d" % (tl % 2)], writes=["r1_%d" % (tl % 2)])
                    ln_a(r1_, "r1_%d" % (tl % 2), tl)

                def lnB(tl):
                    r1_ = r1s[tl % 2]
                    xb_ = x1b[tl % 2]
                    ln_b(r1_, "r1_%d" % (tl % 2), xf[:, tl, :], xfn, tl, R_G1, R_B1)
                    op("pool", lambda h: h.tensor_copy(out=xb_[:], in_=xf[:, tl, :]), reads=[xfn], writes=["x1b%d" % (tl % 2)])

                def xpose(tl):
                    xb_ = x1b[tl % 2]
                    for hf in range(2):
                        bk = banks[2 + hf]
                        for q4 in range(4):
                            kc = hf * 4 + q4
                            op("pe", lambda h: h.matmul(
                                bk[:, q4 * 128:(q4 + 1) * 128], lhsT=xb_[:, kc * 128:(kc + 1) * 128], rhs=identb[:],
                                start=True, stop=True), reads=["x1b%d" % (tl % 2), "identb"], writes=["bank%d" % (2 + hf)])
                        op("dve", lambda h: h.tensor_copy(
                            out=xT[:, hf * 4:hf * 4 + 4, tl, :], in_=bk[:].rearrange("p (a b) -> p a b", a=4)),
                            reads=["bank%d" % (2 + hf)], writes=[xTn])

                def ple(tl, hf):
                    ple_pe(tl, hf)
                    deferred.append(lambda: ple_post(tl, hf))

                def ple_pe(tl, hf):
                    bpe = banks[2]
                    bpg = banks[3]
                    for kc in range(2):
                        op("pe", lambda h: h.matmul(
                            bpe[:], lhsT=pb[:, tl, kc, :], rhs=wpe[:, kc, hf * 512:(hf + 1) * 512],
                            start=(kc == 0), stop=(kc == 1)), reads=[pbn + "_%d" % tl, "wpe"], writes=["bank2"])
                    for kc in range(8):
                        op("pe", lambda h: h.matmul(
                            bpg[:], lhsT=xT[:, kc, tl, :], rhs=wpg[:, kc, hf * 512:(hf + 1) * 512],
                            start=(kc == 0), stop=(kc == 7)), reads=[xTn, "wpg"], writes=["bank3"])

                def ple_post(tl, hf):
                    bpe = banks[2]
                    bpg = banks[3]
                    op("act", lambda h: h.activation(out=sgp[:], in_=bpg[:], func=AF.Exp, scale=-1.0), reads=["bank3"], writes=["sgp"])
                    op("dve", lambda h: h.tensor_scalar(out=sgp[:], in0=sgp[:], scalar1=1.0, scalar2=None, op0=ALU.add),
                       reads=["sgp"], writes=["sgp"])
                    op("dve", lambda h: h.reciprocal(out=sgp[:], in_=sgp[:]), reads=["sgp"], writes=["sgp"])
                    op("dve", lambda h: h.tensor_tensor(out=sgp[:], in0=bpe[:], in1=sgp[:], op=ALU.mult),
                       reads=["bank2", "sgp"], writes=["sgp"])
                    xs = xf[:, tl, hf * 512:(hf + 1) * 512]
                    op("dve", lambda h: h.scalar_tensor_tensor(out=xs, in0=xs, scalar=ALPHA, in1=sgp[:],
                                                               op0=ALU.mult, op1=ALU.add),
                       reads=[xfn, "sgp"], writes=[xfn])
                    op("dve", lambda h: h.tensor_tensor(
                        out=xs, in0=xs, in1=rows[:, R_BDN + hf * 512:R_BDN + (hf + 1) * 512], op=ALU.add),
                        reads=[xfn, "rows"], writes=[xfn])

                O, A, B, X, Pl = outproj, lnA, lnB, xpose, ple
                order = [(loads,), (O, 0, 0), (O, 0, 1), (A, 0), (O, 1, 0), (O, 1, 1), (B, 0), (A, 1), (X, 0),
                         (O, 2, 0), (O, 2, 1), (B, 1), (Pl, 0, 0), (A, 2), (X, 1), (Pl, 0, 1), (O, 3, 0), (O, 3, 1),
                         (B, 2), (Pl, 1, 0), (A, 3), (X, 2), (Pl, 1, 1), (B, 3), (Pl, 2, 0), (X, 3), (Pl, 2, 1),
                         (Pl, 3, 0), (Pl, 3, 1)]
                def run_piece(t):
                    while deferred:
                        deferred.pop(0)()
                    t[0](*t[1:])
                return [(lambda t=t: run_piece(t)) for t in order] + [lambda: run_piece((lambda: None,))]

            def post_pieces(blk):
                par = blk % 2
                xf = x1f[par]
                xfn = "x1f%d" % par

                def pa(tl):
                    ln_a(xf[:, tl, :], xfn, 4 + tl)

                def pb_(tl):
                    ti = blk * 4 + tl
                    ob_ = obuf[ti % 2]
                    obn_ = "obuf%d" % (ti % 2)
                    ln_b(xf[:, tl, :], xfn, ob_[:], obn_, 4 + tl, R_G2, R_B2)
                    dma("pool", "d2", lambda h: h.dma_start(out=out_d[ti], in_=ob_[:]), reads=[obn_], writes=["out%d" % (ti % 2)])
                order = [(pa, 0), (pa, 1), (pb_, 0), (pa, 2), (pb_, 1), (pa, 3), (pb_, 2), (pb_, 3)]
                return [(lambda t=t: t[0](*t[1:])) for t in order]

            upctr = [0]
            dnctr = [0]
            for f in pre_pieces(0):
                f()
            pending_post = []
            for blk in range(8):
                par = blk % 2
                xf, xT = x1f[par], x1T[par]
                xfn, xTn = "x1f%d" % par, "x1T%d" % par
                nxt = pending_post + (pre_pieces(blk + 1) if blk + 1 < 8 else [])
                nxt.reverse()

                def filler(k=1):
                    for _ in range(k):
                        if nxt:
                            nxt.pop()()
                for c4 in range(8):
                    u = upctr[0]
                    upctr[0] += 1
                    wu = wup[u % 2]
                    wun = "wup%d" % (u % 2)
                    dma("sp", "d4", lambda h: h.dma_start(out=wu[:], in_=wupb[:, :, c4 * 512:(c4 + 1) * 512]),
                        reads=["wupb"], writes=[wun])
                    for cs in range(4):
                        c = c4 * 4 + cs
                        bk = banks[c % 2]
                        bn = "bank%d" % (c % 2)
                        for kc in range(8):
                            op("pe", lambda h: h.matmul(
                                bk[:], lhsT=wu[:, kc, cs * 128:(cs + 1) * 128], rhs=xT[:, kc, :, :].rearrange("p a b -> p (a b)"),
                                start=(kc == 0), stop=(kc == 7)), reads=[wun, xTn], writes=[bn])
                        ht = htmp[c % 2]
                        htn = "htmp%d" % (c % 2)
                        op("act", lambda h: h.activation(
                            out=ht[:], in_=bk[:], func=AF.Relu, bias=cols[:, C_BUP + c:C_BUP + c + 1]),
                            reads=[bn, "cols"], writes=[htn])
                        op("act", lambda h: h.activation(out=hT[:, c, :], in_=ht[:], func=AF.Square),
                           reads=[htn], writes=["hT"])
                        filler(1)
                for hf in range(2):
                    for c4 in range(8):
                        u = dnctr[0]
                        dnctr[0] += 1
                        wd = wdn[u % 2]
                        wdn_n = "wdn%d" % (u % 2)
                        dma("sp", "d4", lambda h: h.dma_start(
                            out=wd[:], in_=wdnb[:, c4 * 4:(c4 + 1) * 4, hf * 512:(hf + 1) * 512]), reads=["wdnb"], writes=[wdn_n])
                        for cs in range(4):
                            c = c4 * 4 + cs
                            for tl in range(4):
                                op("pe", lambda h: h.matmul(
                                    banks[4 + tl][:], lhsT=hT[:, c, tl * 128:(tl + 1) * 128], rhs=wd[:, cs, :],
                                    start=(c == 0), stop=(c == 31)), reads=["hT", wdn_n], writes=["bank%d" % (4 + tl)])
                        filler(1)
                    for tl in range(4):
                        xs = xf[:, tl, hf * 512:(hf + 1) * 512]
                        op("dve", lambda h: h.tensor_tensor(out=xs, in0=banks[4 + tl][:], in1=xs, op=ALU.add),
                           reads=["bank%d" % (4 + tl), xfn], writes=[xfn])
                filler(100)
                pending_post = post_pieces(blk)
            for f in pending_post:
                f()
        P.final_wait("sp")
        P.replay()
    P.close()
    return nc


def _chunk_rows(w):
    K, C = w.shape
    return np.ascontiguousarray(w.reshape(K // 128, 128, C).transpose(1, 0, 2))


def _const_tables(h):
    f32 = np.float32
    kl = np.arange(128)[:, None]
    ql = np.arange(128)[None, :]
    wm = np.zeros((128, 6, 128), f32)
    for s in range(6):
        dlt = h + 4 - s
        if dlt < 0 or dlt == 5:
            wm[:, s, :] = -1.0
        elif dlt == 0:
            wm[:, s, :] = np.where(kl > ql, -1.0, 0.0)
        elif dlt == 4:
            wm[:, s, :] = np.where(kl <= ql, -1.0, 0.0)
    dm = np.zeros((128, 2, 128), f32)
    tri = np.where(kl > ql, -1.0, 0.0)
    if h == 0:
        dm[:, 0, :] = tri
        dm[:, 1, :] = -1.0
    else:
        dm[:, 1, :] = tri
    cm = np.zeros((128, 8, 2, 128), f32)
    for m in range(8):
        r = 2 * m + h
        lhs = 16 * kl + 31 - ql
        cm[:, m, 1, :] = np.where(lhs <= 128 * r, 0.0, -1.0)
        cm[:, m, 0, :] = np.where(lhs <= 128 * r + 2048, 0.0, -1.0)
    bonus = np.zeros((NSTEP, 128, 128), f32)
    j = np.arange(128)[None, :]
    for i in range(NSTEP):
        pos = 128 * (2 * i + h) + np.arange(128)[:, None]
        cur = pos // 64
        valid = 64 * j <= pos
        forced = (j == 0) | (j == cur) | (j == cur - 1)
        bonus[i] = np.where(valid, np.where(forced, 1e4, 0.0), -2e30)
    n = np.arange(512)[:, None]
    jj = np.arange(128)[None, :]
    mo = ((16 * n <= 64 * jj + 63) & (16 * n + 31 >= 64 * jj) & (n < 511)).astype(f32)
    mov = np.ascontiguousarray(mo.reshape(4, 128, 128).transpose(1, 0, 2))
    return wm, dm, cm, bonus, mov


def _prep_shared(inp):
    f32 = np.float32
    w_in = np.asarray(inp["w_in"][0], f32)
    b_in = np.asarray(inp["b_in"][0], f32)
    qcols = []
    for hh in range(4):
        for g in range(2):
            qcols += list(1024 + g * 256 + hh * 64 + np.arange(64))
    gcols = []
    for g in range(2):
        for br in range(3):
            for hh in range(4):
                gcols.append(2304 + g * 12 + hh * 3 + br)
    own = np.zeros((1024, 1568), f32)
    own[:, 0:512] = w_in[:, qcols]
    own[:, 512:1536] = w_in[:, 0:1024]
    own[:, 1536:1560] = w_in[:, gcols]
    kvcols = np.concatenate([np.arange(1536, 1664), np.arange(1664, 1792), np.arange(1792, 1920),
                             np.arange(2048, 2176), np.arange(1920, 2048), np.arange(2176, 2304)])
    sh = {}
    sh["w_own"] = _chunk_rows(own)
    sh["w_kv"] = _chunk_rows(w_in[:, kvcols])
    for nm, t in (("k", "cmp_w1_k"), ("v", "cmp_w1_v")):
        w1 = np.asarray(inp[t][0], f32)
        sh["w1" + nm] = np.ascontiguousarray(w1.reshape(32, 64, 256).transpose(1, 0, 2))
    w2k = np.asarray(inp["cmp_w2_k"][0], f32)
    w2kp = np.zeros((128, 2, 2, 128), f32)
    for g in range(2):
        for hc in range(2):
            w2kp[:, g, hc, 64 * g:64 * g + 64] = w2k[hc * 128:(hc + 1) * 128, :]
    sh["w2k"] = w2kp
    sh["w2v"] = _chunk_rows(np.asarray(inp["cmp_w2_v"][0], f32))
    sh["posk"] = np.ascontiguousarray(np.asarray(inp["cmp_pos_k"][0], f32).T)
    sh["posv"] = np.ascontiguousarray(np.asarray(inp["cmp_pos_v"][0], f32).T)
    sh["w_out"] = _chunk_rows(np.asarray(inp["w_out"][0], f32))
    sh["w_pg"] = _chunk_rows(np.asarray(inp["w_pg"][0], f32))
    sh["w_pe"] = _chunk_rows(np.asarray(inp["w_pe"][0], f32))
    sh["w_up"] = _chunk_rows(np.asarray(inp["w_up"][0], f32))
    sh["w_dn"] = _chunk_rows(np.asarray(inp["w_down"][0], f32))
    sh["ident"] = np.eye(128, dtype=f32)
    cols = np.zeros((128, NCOL), f32)
    bq = b_in[qcols]
    for hh in range(4):
        cols[:, C_BQ + hh] = bq[hh * 128:(hh + 1) * 128]
    bkv = b_in[kvcols]
    for cb in range(4):
        cols[:, C_BKV + cb] = bkv[cb * 128:(cb + 1) * 128]
    dww = np.asarray(inp["conv_dw_w"][0], f32)[:, 0, :]
    for cc in range(4):
        sl = slice(cc * 128, (cc + 1) * 128)
        cols[:, C_BA + cc] = b_in[0:512][sl]
        cols[:, C_BG + cc] = b_in[512:1024][sl]
        cols[:, C_DWB + cc] = np.asarray(inp["conv_dw_b"][0], f32)[sl]
        cols[:, C_LNG + cc] = np.asarray(inp["conv_ln_g"][0], f32)[sl]
        cols[:, C_LNB + cc] = np.asarray(inp["conv_ln_b"][0], f32)[sl]
        cols[:, C_DW + cc * 31:C_DW + (cc + 1) * 31] = dww[:, sl].T
    cols[:, C_BUP:C_BUP + 32] = np.asarray(inp["b_up"][0], f32).reshape(32, 128).T
    sh["cols_base"] = cols
    rows = np.zeros((NROW,), f32)
    rows[R_BV:R_BV + 256] = bkv[512:768]
    rows[R_BGT:R_BGT + 24] = b_in[gcols]
    rows[R_BOUT:R_BOUT + D] = inp["b_out"][0]
    rows[R_G1:R_G1 + D] = inp["ln1_g"][0]
    rows[R_B1:R_B1 + D] = inp["ln1_b"][0]
    rows[R_BDN:R_BDN + D] = inp["b_down"][0]
    rows[R_G2:R_G2 + D] = inp["ln2_g"][0]
    rows[R_B2:R_B2 + D] = inp["ln2_b"][0]
    sh["rows"] = np.ascontiguousarray(np.broadcast_to(rows[None, :], (128, NROW)))
    return sh


def _prep_core(inp, sh, c, consts):
    f32 = np.float32
    b, h = c // 2, c % 2
    x = np.asarray(inp["x"][b], f32)
    p = np.asarray(inp["p"][0, b], f32)
    m = {k: v for k, v in sh.items() if k != "cols_base"}
    xTfull = np.ascontiguousarray(x.T)
    m["xT"] = np.ascontiguousarray(xTfull.reshape(8, 128, S).transpose(1, 0, 2))
    xpad = np.concatenate([np.zeros((32, D), f32), x], axis=0)
    xto = np.zeros((NSTEP, 128, 8, 160), f32)
    xo = np.zeros((NSTEP, 128, D), f32)
    pto = np.zeros((NSTEP, 128, 2, 128), f32)
    for i in range(NSTEP):
        qt = 2 * i + h
        seg = xpad[128 * qt:128 * qt + 160]
        xto[i] = seg.T.reshape(8, 128, 160).transpose(1, 0, 2)
        xo[i] = x[128 * qt:128 * qt + 128]
        pto[i] = p[128 * qt:128 * qt + 128].T.reshape(2, 128, 128).transpose(1, 0, 2)
    m["xTo"], m["xo"], m["pTo"] = xto, xo, pto
    cols = sh["cols_base"].copy()
    cols[:, C_HALO] = float(h)
    m["cols"] = cols
    wm, dm, cm, bonus, mov = consts[h]
    m["wmask"], m["dmask"], m["cmask"], m["bonus"], m["mov"] = wm, dm, cm, bonus, mov
    return m


_CACHE = {}


def kernel(**inputs):
    if "nc" not in _CACHE:
        _CACHE["nc"] = build_program()
        _CACHE["consts"] = {h: _const_tables(h) for h in range(2)}
    nc = _CACHE["nc"]
    sh = _prep_shared(inputs)
    in_maps = [_prep_core(inputs, sh, c, _CACHE["consts"]) for c in range(NCORES)]
    res = run_bass_kernel_spmd(nc, in_maps, core_ids=list(range(NCORES)))
    out = np.zeros((4, S, D), np.float32)
    for c in range(NCORES):
        b, h = c // 2, c % 2
        o = np.asarray(res.results[c]["out"]).reshape(NSTEP, 128, D)
        for i in range(NSTEP):
            qt = 2 * i + h
            out[b, 128 * qt:128 * qt + 128] = o[i]
    return out
```

```python
import contextlib
import numpy as np
import concourse.bass as bass
import concourse.mybir as mybir
from concourse.bass_utils import run_bass_kernel_spmd

F32 = mybir.dt.float32
BF16 = mybir.dt.bfloat16
AF = mybir.ActivationFunctionType
ALU = mybir.AluOpType

NCORES = 8
S = 8192
D = 1024
NT_ALL = 64
NSTEP = 32
DFF = 4096
ALPHA = 2.0 ** 0.25
EPS = 1e-5
MASKV = 30000.0
ENGS = ["pe", "act", "dve", "pool", "sp"]
SEM_LIMIT = 24000


class _Rec:
    def __init__(self):
        self.call = None

    def __getattr__(self, name):
        def f(*a, **k):
            self.call = (name, a, k)
            return self
        return f


class Prog:
    def __init__(self, nc):
        self.nc = nc
        self.lists = {e: [] for e in ENGS}
        self.cnt = {}
        self.sems = {}
        self.cur = {}
        self.epoch = {}
        self.waited = {e: {} for e in ENGS}
        self.last_w = {}
        self.readers = {}
        self._stack = []
        for e in ENGS:
            self._new_epoch("e_" + e)

    def _new_epoch(self, base):
        ep = self.epoch.get(base, -1) + 1
        self.epoch[base] = ep
        key = "%s_%d" % (base, ep)
        cm = self.nc.semaphore(key)
        h = cm.__enter__()
        self._stack.append(cm)
        self.sems[key] = h
        self.cnt[key] = 0
        self.cur[base] = key

    def close(self):
        for cm in reversed(self._stack):
            cm.__exit__(None, None, None)

    def _deps(self, eng, reads, writes):
        need = {}
        mine = "e_" + eng + "_"

        def add(tok, same_ok):
            sk, v = tok
            if same_ok and sk.startswith(mine):
                return
            if v > need.get(sk, 0):
                need[sk] = v

        for b in reads:
            lw = self.last_w.get(b)
            if lw is not None:
                add(lw, eng == "pe")
        for b in writes:
            lw = self.last_w.get(b)
            if lw is not None:
                add(lw, True)
            for sk, v in self.readers.get(b, {}).items():
                add((sk, v), True)
        waits = []
        w = self.waited[eng]
        for sk, v in need.items():
            if w.get(sk, 0) < v:
                w[sk] = v
                waits.append((sk, v))
        return waits

    def _record(self, eng, waits, fn, base, inc):
        if self.cnt[self.cur[base]] + inc > SEM_LIMIT:
            self._new_epoch(base)
        key = self.cur[base]
        self.cnt[key] += inc
        val = self.cnt[key]
        ws = [(self.sems[sk], v) for sk, v in waits]
        sh = self.sems[key]
        rec = _Rec()
        fn(rec)
        name, args, kwargs = rec.call

        def emit(h):
            for s, v in ws:
                h.wait_ge(s, v)
            getattr(h, name)(*args, **kwargs).then_inc(sh, inc)

        self.lists[eng].append(emit)
        return (key, val)

    def _mark(self, tok, reads, writes):
        for b in reads:
            r = self.readers.setdefault(b, {})
            if r.get(tok[0], 0) < tok[1]:
                r[tok[0]] = tok[1]
        for b in writes:
            self.last_w[b] = tok
            self.readers[b] = {}

    def op(self, eng, fn, reads=(), writes=()):
        waits = self._deps(eng, reads, writes)
        tok = self._record(eng, waits, fn, "e_" + eng, 1)
        self._mark(tok, reads, writes)

    def dma(self, queue, dsem, fn, reads=(), writes=()):
        base = "dm_" + writes[0]
        if base not in self.cur:
            self._new_epoch(base)
        waits = self._deps(queue, reads, writes)
        tok = self._record(queue, waits, fn, base, 16)
        self._mark(tok, reads, writes)

    def barrier(self):
        snap = [(k, v) for k, v in self.cnt.items() if v > 0]
        for e in ENGS:
            waits = []
            for sk, v in snap:
                if self.waited[e].get(sk, 0) < v and not sk.startswith("e_" + e + "_"):
                    self.waited[e][sk] = v
                    waits.append((self.sems[sk], v))
            if waits:
                def emit(h, waits=waits):
                    for s, v in waits:
                        h.wait_ge(s, v)
                self.lists[e].append(emit)
        self.last_w = {}
        self.readers = {}

    def final_wait(self, eng="sp"):
        waits = [(self.sems[sk], v) for sk, v in self.cnt.items()
                 if v > 0 and not sk.startswith("e_" + eng + "_")]

        def emit(h):
            for s, v in waits:
                h.wait_ge(s, v)
        self.lists[eng].append(emit)

    def replay(self):
        lists = self.lists
        with self.nc.Block() as block:
            @block.tensor
            def _(h):
                for f in lists["pe"]:
                    f(h)

            @block.scalar
            def _(h):
                for f in lists["act"]:
                    f(h)

            @block.vector
            def _(h):
                for f in lists["dve"]:
                    f(h)

            @block.gpsimd
            def _(h):
                for f in lists["pool"]:
                    f(h)

            @block.sync
            def _(h):
                for f in lists["sp"]:
                    f(h)


C_BQ = 0
C_BKV = 4
C_BA = 8
C_BG = 12
C_DWB = 16
C_LNG = 20
C_LNB = 24
C_HALO = 28
C_BUP = 32
C_DW = 64
NCOL = 64 + 124
R_BV = 0
R_BGT = 256
R_BOUT = 288
R_G1 = R_BOUT + 1024
R_B1 = R_G1 + 1024
R_BDN = R_B1 + 1024
R_G2 = R_BDN + 1024
R_B2 = R_G2 + 1024
NROW = R_B2 + 1024


def build_program(debug=False):
    nc = bass.Bass("TRN2", target_bir_lowering=False, dynamic_dma_scratch_size=4096)

    def din(name, shape, dt=F32):
        return nc.dram_tensor(name, list(shape), dt, kind="ExternalInput").ap()

    xT = din("xT", [128, 8, S])
    xTo = din("xTo", [NSTEP, 128, 8, 160])
    xo = din("xo", [NSTEP, 128, D])
    pTo = din("pTo", [NSTEP, 128, 2, 128])
    w_kv = din("w_kv", [128, 8, 768])
    w_own = din("w_own", [128, 8, 1568])
    w1k = din("w1k", [64, 32, 256])
    w1v = din("w1v", [64, 32, 256])
    w2k = din("w2k", [128, 2, 2, 128])
    w2v = din("w2v", [128, 2, 64])
    posk = din("posk", [64, 32])
    posv = din("posv", [64, 32])
    w_out = din("w_out", [128, 8, D])
    w_pg = din("w_pg", [128, 8, D])
    w_pe = din("w_pe", [128, 2, D])
    w_up = din("w_up", [128, 8, DFF])
    w_dn = din("w_dn", [128, 32, D])
    cols_d = din("cols", [128, NCOL])
    rows_d = din("rows", [128, NROW])
    mov_d = din("mov", [128, 4, 128])
    wmask_d = din("wmask", [128, 6, 128])
    dmask_d = din("dmask", [128, 2, 128])
    cmask_d = din("cmask", [128, 8, 2, 128])
    bonus_d = din("bonus", [NSTEP, 128, 128])
    ident_d = din("ident", [128, 128])
    out_d = nc.dram_tensor("out", [NSTEP, 128, D], F32, kind="ExternalOutput").ap()
    ystash = nc.dram_tensor("ystash", [NSTEP, 128, 8, 128], BF16, **({"kind": "ExternalOutput"} if debug else {})).ap()
    dbg = {}
    if debug:
        for nm, shp in (("ksT", [128, S]), ("kwT", [128, S]), ("vsw", [128, NT_ALL, 4, 65]), ("kct", [128, 512]),
                        ("vct", [128, 4, 2, 65]), ("kcT", [128, S])):
            dbg[nm] = nc.dram_tensor("dbg_" + nm, shp, BF16, kind="ExternalOutput").ap()

    wupb = nc.dram_tensor("wupb", [128, 8, DFF], BF16, **({"kind": "ExternalOutput"} if debug else {})).ap()
    wdnb = nc.dram_tensor("wdnb", [128, 32, D], BF16, **({"kind": "ExternalOutput"} if debug else {})).ap()

    P = Prog(nc)
    op = P.op
    dma = P.dma

    with contextlib.ExitStack() as top:
        def sbt(stack, name, shape, dt):
            return stack.enter_context(nc.sbuf_tensor("s_" + name, list(shape), dt))

        banks = [top.enter_context(nc.psum_tensor("bank%d" % i, [128, 512], F32)) for i in range(8)]

        cols = sbt(top, "cols", [128, NCOL], F32)
        rows = sbt(top, "rows", [128, NROW], F32)
        identf = sbt(top, "identf", [128, 128], F32)
        identb = sbt(top, "identb", [128, 128], BF16)
        i30k = sbt(top, "i30k", [128, 128], BF16)
        onesf = sbt(top, "onesf", [128, 128], F32)

        dma("sp", "d1", lambda h: h.dma_start(out=cols[:], in_=cols_d), writes=["cols"])
        dma("sp", "d1", lambda h: h.dma_start(out=rows[:], in_=rows_d), writes=["rows"])
        dma("sp", "d1", lambda h: h.dma_start(out=identf[:], in_=ident_d), writes=["identf"])
        op("act", lambda h: h.activation(out=identb[:], in_=identf[:], func=AF.Copy), reads=["identf"], writes=["identb"])
        op("dve", lambda h: h.tensor_scalar(out=i30k[:], in0=identf[:], scalar1=MASKV, scalar2=None, op0=ALU.mult), reads=["identf"], writes=["i30k"])
        op("dve", lambda h: h.memset(onesf[:], 1.0 / 512.0), writes=["onesf"])
        ncols = sbt(top, "ncols", [128, 16], F32)
        op("dve", lambda h: h.tensor_scalar(out=ncols[:], in0=cols[:, C_BG:C_BG + 16], scalar1=-1.0, scalar2=None, op0=ALU.mult),
           reads=["cols"], writes=["ncols"])

        with contextlib.ExitStack() as p01:
            ksT = sbt(p01, "ksT", [128, S], BF16)
            kwT = sbt(p01, "kwT", [128, S], BF16)
            vsw = sbt(p01, "vsw", [128, NT_ALL, 4, 65], BF16)
            kct = sbt(p01, "kct", [128, 512], BF16)
            vct = sbt(p01, "vct", [128, 4, 2, 65], BF16)
            op("dve", lambda h: h.memset(vsw[:, :, :, 64:65], 1.0), writes=["vsw1"])
            op("dve", lambda h: h.memset(vct[:], 0.0), writes=["vct"])
            op("dve", lambda h: h.memset(vct[:, :, :, 64:65], 1.0), writes=["vct"])
            op("dve", lambda h: h.memset(kct[:], 0.0), writes=["kct"])

            with contextlib.ExitStack() as p0:
                wkv = sbt(p0, "wkv", [128, 8, 768], BF16)
                kcT = sbt(p0, "kcT", [128, S], BF16)
                vcT = sbt(p0, "vcT", [128, S], BF16)
                w1 = [sbt(p0, "w1k", [128, 32, 256], BF16), sbt(p0, "w1v", [128, 32, 256], BF16)]
                w2ks = sbt(p0, "w2ks", [128, 2, 2, 128], BF16)
                w2vs = sbt(p0, "w2vs", [128, 2, 64], BF16)
                pos = [sbt(p0, "posk", [128, 32], BF16), sbt(p0, "posv", [128, 32], BF16)]
                xc = [sbt(p0, "xc0", [128, 8, 512], BF16), sbt(p0, "xc1", [128, 8, 512], BF16)]
                h1T = sbt(p0, "h1T", [128, 2, 512], BF16)
                cvec = sbt(p0, "cvec", [128, 2], F32)
                zt0 = sbt(p0, "zt0", [128, 512], F32)
                et0 = sbt(p0, "et0", [128, 512], F32)

                dma("pool", "d0", lambda h: h.dma_start(out=wkv[:], in_=w_kv), writes=["wkv"])
                for j in range(2):
                    dma("pool", "d0", lambda h, j=j: h.dma_start(out=xc[j][:], in_=xT[:, :, j * 512:(j + 1) * 512]),
                        writes=["xc%d" % j])
                for t, (wd, pd) in enumerate([(w1k, posk), (w1v, posv)]):
                    for half in range(2):
                        dma("pool", "d0", lambda h, t=t, wd=wd, half=half: h.dma_start(
                            out=w1[t][64 * half:64 * half + 64, :, :], in_=wd), writes=["w1_%d" % t])
                        dma("pool", "d0", lambda h, t=t, pd=pd, half=half: h.dma_start(
                            out=pos[t][64 * half:64 * half + 64, :], in_=pd), writes=["pos%d" % t])
                dma("pool", "d0", lambda h: h.dma_start(out=w2ks[:], in_=w2k), writes=["w2ks"])
                dma("pool", "d0", lambda h: h.dma_start(out=w2vs[:], in_=w2v), writes=["w2vs"])

                kdst = [kcT, vcT, ksT, kwT]
                for j in range(16):
                    xj = xc[j % 2]
                    xn = "xc%d" % (j % 2)
                    for cb in range(4):
                        bk = banks[cb % 2]
                        bn = "bank%d" % (cb % 2)
                        for k in range(8):
                            op("pe", lambda h, bk=bk, cb=cb, k=k, xj=xj: h.matmul(
                                bk[:], lhsT=wkv[:, k, cb * 128:(cb + 1) * 128], rhs=xj[:, k, :],
                                start=(k == 0), stop=(k == 7)), reads=["wkv", xn], writes=[bn])
                        if cb < 2:
                            op("act", lambda h, bk=bk, cb=cb, j=j: h.activation(
                                out=kdst[cb][:].rearrange("p (s n) -> p s n", s=16)[:, :, j * 32:(j + 1) * 32],
                                in_=bk[:].rearrange("p (n s) -> p s n", s=16), func=AF.Identity,
                                bias=cols[:, C_BKV + cb:C_BKV + cb + 1]), reads=[bn, "cols"], writes=["kd%d" % cb])
                        else:
                            op("act", lambda h, bk=bk, cb=cb, j=j: h.activation(
                                out=kdst[cb][:, j * 512:(j + 1) * 512], in_=bk[:], func=AF.Identity,
                                bias=cols[:, C_BKV + cb:C_BKV + cb + 1]), reads=[bn, "cols"], writes=["kd%d" % cb])
                    for tt in range(4):
                        bk = banks[2 + tt % 2]
                        bn = "bank%d" % (2 + tt % 2)
                        for k in range(8):
                            op("pe", lambda h, bk=bk, tt=tt, k=k, xj=xj: h.matmul(
                                bk[:, 0:256], lhsT=xj[:, k, tt * 128:(tt + 1) * 128], rhs=wkv[:, k, 512:768],
                                start=(k == 0), stop=(k == 7)), reads=["wkv", xn], writes=[bn])
                        op("dve", lambda h, bk=bk, tt=tt, j=j: h.tensor_tensor(
                            out=vsw[:, j * 4 + tt, :, 0:64],
                            in0=bk[:, 0:256].rearrange("p (a b) -> p a b", a=4),
                            in1=rows[:, R_BV:R_BV + 256].rearrange("p (a b) -> p a b", a=4), op=ALU.add),
                            reads=[bn, "rows"], writes=["vsw"])
                    if j + 2 < 16:
                        dma("pool", "d0", lambda h, j=j: h.dma_start(
                            out=xc[j % 2][:], in_=xT[:, :, (j + 2) * 512:(j + 3) * 512]), writes=[xn])

                for t in range(2):
                    src = kcT if t == 0 else vcT
                    srcn = "kd%d" % t
                    for g in range(2):
                        pr = slice(64 * g, 64 * g + 64)
                        for hc in range(2):
                            bc = banks[4]
                            for l in range(32):
                                op("pe", lambda h, l=l, hc=hc, t=t, pr=pr, bc=bc: h.matmul(
                                    bc[:, 0:1], lhsT=w1[t][pr, l, hc * 128:(hc + 1) * 128], rhs=pos[t][pr, l:l + 1],
                                    start=(l == 0), stop=(l == 31)), reads=["w1_%d" % t, "pos%d" % t], writes=["bank4"])
                            op("dve", lambda h, hc=hc, bc=bc: h.tensor_copy(out=cvec[:, hc:hc + 1], in_=bc[:, 0:1]),
                               reads=["bank4"], writes=["cvec"])
                            bh = banks[5 + hc]
                            bhn = "bank%d" % (5 + hc)
                            for l in range(32):
                                op("pe", lambda h, l=l, hc=hc, t=t, pr=pr, bh=bh, src=src: h.matmul(
                                    bh[:, 0:511], lhsT=w1[t][pr, l, hc * 128:(hc + 1) * 128],
                                    rhs=src[pr, :].rearrange("p (s n) -> p s n", s=16)[:, l % 16, (l // 16):(l // 16) + 511],
                                    start=(l == 0), stop=(l == 31)), reads=["w1_%d" % t, srcn], writes=[bhn])
                            op("dve", lambda h, hc=hc, bh=bh: h.tensor_scalar(
                                out=zt0[:, 0:511], in0=bh[:, 0:511], scalar1=cvec[:, hc:hc + 1], scalar2=None, op0=ALU.add),
                                reads=[bhn, "cvec"], writes=["zt0"])
                            op("act", lambda h: h.activation(out=et0[:, 0:511], in_=zt0[:, 0:511], func=AF.Exp, scale=-1.0),
                               reads=["zt0"], writes=["et0"])
                            op("dve", lambda h: h.tensor_scalar(out=et0[:, 0:511], in0=et0[:, 0:511], scalar1=1.0, scalar2=None, op0=ALU.add),
                               reads=["et0"], writes=["et0"])
                            op("dve", lambda h: h.reciprocal(out=et0[:, 0:511], in_=et0[:, 0:511]), reads=["et0"], writes=["et0"])
                            op("dve", lambda h, hc=hc: h.tensor_tensor(out=h1T[:, hc, 0:511], in0=zt0[:, 0:511], in1=et0[:, 0:511], op=ALU.mult),
                               reads=["zt0", "et0"], writes=["h1T"])
                        if t == 0:
                            bo = banks[7]
                            for hc in range(2):
                                op("pe", lambda h, hc=hc, g=g, bo=bo: h.matmul(
                                    bo[:, 0:511], lhsT=w2ks[:, g, hc, :], rhs=h1T[:, hc, 0:511],
                                    start=(hc == 0), stop=(hc == 1)), reads=["w2ks", "h1T"], writes=["bank7"])
                            op("act", lambda h, pr=pr, bo=bo: h.activation(
                                out=kct[pr, 0:511], in_=bo[pr, 0:511], func=AF.Copy), reads=["bank7"], writes=["kct"])
                        else:
                            for nt in range(4):
                                nn = 128 if nt < 3 else 127
                                bo = banks[7]
                                for hc in range(2):
                                    op("pe", lambda h, hc=hc, nt=nt, nn=nn, bo=bo: h.matmul(
                                        bo[0:nn, 0:64], lhsT=h1T[:, hc, nt * 128:nt * 128 + nn], rhs=w2vs[:, hc, :],
                                        start=(hc == 0), stop=(hc == 1)), reads=["w2vs", "h1T"], writes=["bank7"])
                                op("act", lambda h, nt=nt, nn=nn, g=g, bo=bo: h.activation(
                                    out=vct[0:nn, nt, g, 0:64], in_=bo[0:nn, 0:64], func=AF.Copy),
                                    reads=["bank7"], writes=["vct"])
                if debug:
                    for nm, t in (("ksT", ksT), ("kwT", kwT), ("vsw", vsw), ("kct", kct), ("vct", vct), ("kcT", kcT)):
                        dma("sp", "d5", lambda h, nm=nm, t=t: h.dma_start(out=dbg[nm], in_=t[:]),
                            reads=["kd0", "kd1", "kd2", "kd3", "vsw", "vsw1", "kct", "vct"], writes=["dbg" + nm])
            P.barrier()

            with contextlib.ExitStack() as p1:
                wown = sbt(p1, "wown", [128, 8, 1568], BF16)
                diag = sbt(p1, "diag", [128, 4, 31, 128], BF16)
                b30k = sbt(p1, "b30k", [128, S], BF16)
                mov = sbt(p1, "mov", [128, 4, 128], BF16)
                wmask = sbt(p1, "wmask", [128, 6, 128], BF16)
                dmask = sbt(p1, "dmask", [128, 2, 128], BF16)
                cmask = sbt(p1, "cmask", [128, 8, 2, 128], BF16)
                xt = [sbt(p1, "xt0", [128, 8, 160], BF16), sbt(p1, "xt1", [128, 8, 160], BF16)]
                bon = [sbt(p1, "bon0", [128, 128], F32), sbt(p1, "bon1", [128, 128], F32)]
                QTz4 = [[sbt(p1, "QTz%d_%d" % (sp_, g), [128, 4, 128], BF16) for g in range(2)] for sp_ in range(2)]
                gsigs = [sbt(p1, "gsig0", [128, 32], F32), sbt(p1, "gsig1", [128, 32], F32)]
                sig = [sbt(p1, "sig0", [128, 160], F32), sbt(p1, "sig1", [128, 160], F32)]
                vT = sbt(p1, "vT", [128, 4, 160], BF16)
                yconv = sbt(p1, "yconv", [128, 4, 128], F32)
                ysq = sbt(p1, "ysq", [128, 4, 128], F32)
                mean_sb = sbt(p1, "mean_sb", [128, 128], F32)
                var_sb = sbt(p1, "var_sb", [128, 128], F32)
                rstd = sbt(p1, "rstd", [128, 128], F32)
                mr = sbt(p1, "mr", [128, 128], F32)
                tnorm = [sbt(p1, "tn0", [128, 128], F32), sbt(p1, "tn1", [128, 128], F32)]
                etn = [sbt(p1, "etn0", [128, 128], F32), sbt(p1, "etn1", [128, 128], F32)]
                yT = [sbt(p1, "yT0", [128, 8, 128], BF16), sbt(p1, "yT1", [128, 8, 128], BF16)]
                PcT = [sbt(p1, "PcT0", [128, 4, 512], BF16), sbt(p1, "PcT1", [128, 4, 512], BF16)]
                PT = [sbt(p1, "PT%d" % k, [128, 512], BF16) for k in range(4)]
                Osb = [sbt(p1, "Osb0", [65, 512], F32), sbt(p1, "Osb1", [65, 512], F32)]
                rd = sbt(p1, "rd", [128, 4], F32)
                rdc = [sbt(p1, "rdc0", [128, 4], F32), sbt(p1, "rdc1", [128, 4], F32)]
                wgt = sbt(p1, "wgt", [128, 4], F32)
                imp = sbt(p1, "imp", [128, 128], F32)
                score = sbt(p1, "score", [128, 128], F32)
                m8a = sbt(p1, "m8a", [128, 8], F32)
                m8b = sbt(p1, "m8b", [128, 8], F32)
                thr = sbt(p1, "thr", [128, 1], F32)
                negs = [sbt(p1, "neg0", [128, 128], BF16), sbt(p1, "neg1", [128, 128], BF16)]
                negT = [sbt(p1, "negT0", [128, 128], BF16), sbt(p1, "negT1", [128, 128], BF16)]
                yacc = sbt(p1, "yacc", [128, 8, 64], F32)
                ytmp = sbt(p1, "ytmp", [128, 4, 64], F32)
                ynb = sbt(p1, "ynb", [128, 512], BF16)

                dma("pool", "d0", lambda h: h.dma_start(out=wown[:], in_=w_own), writes=["wown"])
                dma("pool", "d0", lambda h: h.dma_start(out=mov[:], in_=mov_d), writes=["mov"])
                dma("pool", "d0", lambda h: h.dma_start(out=wmask[:], in_=wmask_d), writes=["wmask"])
                dma("pool", "d0", lambda h: h.dma_start(out=dmask[:], in_=dmask_d), writes=["dmask"])
                dma("pool", "d0", lambda h: h.dma_start(out=cmask[:], in_=cmask_d), writes=["cmask"])
                op("pool", lambda h: h.memset(b30k[:], MASKV), writes=["b30k"])
                op("pool", lambda h: h.affine_select(out=b30k[:], in_=b30k[:], pattern=[[1, S]], compare_op=ALU.is_ge,
                                                     fill=0.0, base=0, channel_multiplier=-64),
                   reads=["b30k"], writes=["b30k"])
                op("pool", lambda h: h.affine_select(out=b30k[:], in_=b30k[:], pattern=[[-1, S]], compare_op=ALU.is_ge,
                                                     fill=0.0, base=63, channel_multiplier=64),
                   reads=["b30k"], writes=["b30k"])
                for cc in range(4):
                    for j in range(31):
                        op("dve", lambda h, cc=cc, j=j: h.tensor_scalar(
                            out=diag[:, cc, j, :], in0=identf[:], scalar1=cols[:, C_DW + cc * 31 + j:C_DW + cc * 31 + j + 1],
                            scalar2=None, op0=ALU.mult), reads=["identf", "cols"], writes=["diag"])

                def load_x(i):
                    dma("pool", "d3", lambda h, i=i: h.dma_start(out=xt[i % 2][:], in_=xTo[i]), writes=["xt%d" % (i % 2)])

                def load_bon(i):
                    dma("sp", "d1", lambda h, i=i: h.dma_start(out=bon[i % 2][:], in_=bonus_d[i]), writes=["bon%d" % (i % 2)])

                for g in range(2):
                    for sp_ in range(2):
                        op("dve", lambda h: h.memset(QTz4[sp_][g][:], 0.0), writes=["QTz%d_%d" % (sp_, g)])
                load_x(0)
                load_x(1)
                load_bon(0)
                sctr = [0]
                octr = [0]

                def mk_unit(lhsK, kn, g, masks, vlhs, vn, ob, obn, first, last, pdst=None, pdn=None, qpar=0):
                    u = sctr[0]
                    sctr[0] += 1
                    sb_ = banks[(0, 1, 6)[u % 3]]
                    sbn = "bank%d" % ((0, 1, 6)[u % 3])
                    if pdst is None:
                        pt = PT[u % 4][:]
                        ptn = "PT%d" % (u % 4)
                    else:
                        pt, ptn = pdst, pdn
                    pr = slice(64 * g, 64 * g + 64)
                    nm = len(masks)

                    def s_():
                        op("pe", lambda h: h.matmul(sb_[:], lhsT=lhsK, rhs=QTz4[qpar][g][:].rearrange("p a b -> p (a b)"),
                                                    start=True, stop=(nm == 0)), reads=[kn, "QTz%d_%d" % (qpar, g)], writes=[sbn])
                        for mi, (ml, mr_, mrn) in enumerate(masks):
                            op("pe", lambda h: h.matmul(
                                sb_[:].rearrange("p (a b) -> p a b", a=4), lhsT=ml,
                                rhs=mr_.unsqueeze(1).broadcast_to([128, 4, 128]), start=False, stop=(mi == nm - 1)),
                                reads=[mrn, "b30k", "i30k"], writes=[sbn])

                    def e_():
                        op("act", lambda h: h.activation(out=pt, in_=sb_[:], func=AF.Exp, scale=0.125),
                           reads=[sbn], writes=[ptn])

                    def pv_():
                        op("pe", lambda h: h.matmul(ob[0:65, :], lhsT=vlhs, rhs=pt, start=first, stop=last),
                           reads=[vn, ptn], writes=[obn])
                    return {"s": s_, "e": e_, "pv": pv_, "after": []}

                def run_units(units):
                    pend = []
                    n = len(units)
                    for k in range(min(2, n)):
                        units[k]["s"]()
                    for u in range(n):
                        if u + 2 < n:
                            units[u + 2]["s"]()
                        units[u]["e"]()
                        units[u]["pv"]()
                        nxt = []
                        for cnt, f in pend:
                            if cnt <= 1:
                                f()
                            else:
                                nxt.append((cnt - 1, f))
                        pend = nxt
                        for dly, f in units[u]["after"]:
                            if dly == 0:
                                f()
                            else:
                                pend.append((dly, f))
                    for _, f in pend:
                        f()

                def fin1(ob, obn):
                    k = octr[0] % 2
                    octr[0] += 1
                    osb = Osb[k]
                    osn = "Osb%d" % k
                    op("act", lambda h: h.activation(out=osb[:], in_=ob[0:65, :], func=AF.Copy), reads=[obn], writes=[osn])
                    return osb, osn

                def fin2(osb, osn, g, br, gsig, gsn):
                    ot = banks[3]
                    otv = ot[:, 192:452].rearrange("p (a b) -> p a b", a=4)
                    for hh in range(4):
                        op("pe", lambda h: h.transpose(out=ot[:, 192 + hh * 65:192 + (hh + 1) * 65],
                                                       in_=osb[:, hh * 128:(hh + 1) * 128], identity=identf[0:65, 0:65]),
                           reads=[osn, "identf"], writes=["bank3"])
                    op("dve", lambda h: h.tensor_scalar(out=rd[:], in0=otv[:, :, 64], scalar1=1e-30, scalar2=None, op0=ALU.max),
                       reads=["bank3"], writes=["rd"])
                    op("dve", lambda h: h.reciprocal(out=rd[:], in_=rd[:]), reads=["rd"], writes=["rd"])
                    if br == 0:
                        op("dve", lambda h: h.tensor_copy(out=rdc[g][:], in_=rd[:]), reads=["rd"], writes=["rdc%d" % g])
                    gc = g * 12 + br * 4
                    op("dve", lambda h: h.tensor_tensor(out=wgt[:], in0=rd[:], in1=gsig[:, gc:gc + 4], op=ALU.mult),
                       reads=["rd", gsn], writes=["wgt"])
                    wb = wgt[:].unsqueeze(2).broadcast_to([128, 4, 64])
                    if br == 0:
                        op("dve", lambda h: h.tensor_tensor(out=yacc[:, 4 * g:4 * g + 4, :], in0=otv[:, :, 0:64], in1=wb, op=ALU.mult),
                           reads=["bank3", "wgt"], writes=["yacc"])
                    else:
                        op("dve", lambda h: h.tensor_tensor(out=ytmp[:], in0=otv[:, :, 0:64], in1=wb, op=ALU.mult),
                           reads=["bank3", "wgt"], writes=["ytmp"])
                        op("dve", lambda h: h.tensor_tensor(out=yacc[:, 4 * g:4 * g + 4, :], in0=yacc[:, 4 * g:4 * g + 4, :],
                                                            in1=ytmp[:], op=ALU.add),
                           reads=["yacc", "ytmp"], writes=["yacc"])

                def topk_chain(g, bo_, bon_n, bu, bun):
                    op("dve", lambda h: h.tensor_scalar(out=imp[:], in0=bu[:, 0:128], scalar1=rdc[g][:, 0:1], scalar2=None, op0=ALU.mult),
                       reads=[bun, "rdc%d" % g], writes=["imp"])
                    for hh in range(1, 4):
                        op("dve", lambda h: h.scalar_tensor_tensor(
                            out=imp[:], in0=bu[:, hh * 128:(hh + 1) * 128], scalar=rdc[g][:, hh:hh + 1], in1=imp[:],
                            op0=ALU.mult, op1=ALU.add), reads=[bun, "rdc%d" % g, "imp"], writes=["imp"])
                    op("dve", lambda h: h.tensor_tensor(out=score[:], in0=imp[:], in1=bo_[:], op=ALU.add),
                       reads=["imp", bon_n], writes=["score"])
                    op("dve", lambda h: h.max(out=m8a[:], in_=score[:]), reads=["score"], writes=["m8a"])
                    op("dve", lambda h: h.match_replace(out=imp[:], in_to_replace=m8a[:], in_values=score[:], imm_value=-3e30),
                       reads=["score", "m8a"], writes=["imp"])
                    op("dve", lambda h: h.max(out=m8b[:], in_=imp[:]), reads=["imp"], writes=["m8b"])
                    op("dve", lambda h: h.tensor_scalar(out=thr[:], in0=m8b[:, 7:8], scalar1=-1e30, scalar2=None, op0=ALU.max),
                       reads=["m8b"], writes=["thr"])
                    op("dve", lambda h: h.tensor_scalar(out=negs[g][:], in0=score[:], scalar1=thr[:, 0:1], scalar2=1.0,
                                                        op0=ALU.is_ge, op1=ALU.subtract),
                       reads=["score", "thr"], writes=["neg%d" % g])

                tail = [None]

                def stage1(j):
                    x_ = xt[j % 2]
                    xn = "xt%d" % (j % 2)
                    gsig = gsigs[j % 2]
                    gsn = "gsig%d" % (j % 2)
                    bq = banks[2]
                    for hh in range(4):
                        for k in range(8):
                            op("pe", lambda h: h.matmul(
                                bq[:, hh * 128:(hh + 1) * 128], lhsT=wown[:, k, hh * 128:(hh + 1) * 128], rhs=x_[:, k, 32:160],
                                start=(k == 0), stop=(k == 7)), reads=["wown", xn], writes=["bank2"])
                    bg = banks[3]
                    for k in range(8):
                        op("pe", lambda h: h.matmul(bg[:, 0:32], lhsT=x_[:, k, 32:160], rhs=wown[:, k, 1536:1568],
                                                    start=(k == 0), stop=(k == 7)), reads=["wown", xn], writes=["bank3"])
                    for g in range(2):
                        pr = slice(64 * g, 64 * g + 64)
                        op("dve", lambda h: h.tensor_tensor(
                            out=QTz4[j % 2][g][pr, :, :], in0=bq[pr, :].rearrange("p (a b) -> p a b", a=4),
                            in1=cols[pr, C_BQ:C_BQ + 4].unsqueeze(2).broadcast_to([64, 4, 128]), op=ALU.add),
                            reads=["bank2", "cols"], writes=["QTz%d_%d" % (j % 2, g)])
                    op("dve", lambda h: h.tensor_tensor(out=gsig[:], in0=bg[:, 0:32], in1=rows[:, R_BGT:R_BGT + 32], op=ALU.add),
                       reads=["bank3", "rows"], writes=[gsn])
                    op("act", lambda h: h.activation(out=gsig[:], in_=gsig[:], func=AF.Exp, scale=-1.0), reads=[gsn], writes=[gsn])
                    op("dve", lambda h: h.tensor_scalar(out=gsig[:], in0=gsig[:], scalar1=1.0, scalar2=None, op0=ALU.add),
                       reads=[gsn], writes=[gsn])
                    op("dve", lambda h: h.reciprocal(out=gsig[:], in_=gsig[:]), reads=[gsn], writes=[gsn])

                def conv_proj(j, cc):
                    x_ = xt[j % 2]
                    xn = "xt%d" % (j % 2)
                    ba = banks[2]
                    bgp = banks[3]
                    for k in range(8):
                        op("pe", lambda h: h.matmul(
                            ba[:, 0:160], lhsT=wown[:, k, 512 + cc * 128:512 + (cc + 1) * 128], rhs=x_[:, k, :],
                            start=(k == 0), stop=(k == 7)), reads=["wown", xn], writes=["bank2"])
                    for k in range(8):
                        op("pe", lambda h: h.matmul(
                            bgp[:, 0:160], lhsT=wown[:, k, 1024 + cc * 128:1024 + (cc + 1) * 128], rhs=x_[:, k, :],
                            start=(k == 0), stop=(k == 7)), reads=["wown", xn], writes=["bank3"])
                    sg = sig[cc % 2]
                    sgn = "sig%d" % (cc % 2)
                    op("act", lambda h: h.activation(
                        out=sg[:], in_=bgp[:, 0:160], func=AF.Exp, bias=ncols[:, cc:cc + 1], scale=-1.0),
                        reads=["bank3", "ncols"], writes=[sgn])
                    op("dve", lambda h: h.tensor_scalar(out=sg[:], in0=sg[:], scalar1=1.0, scalar2=None, op0=ALU.add),
                       reads=[sgn], writes=[sgn])
                    op("dve", lambda h: h.reciprocal(out=sg[:], in_=sg[:]), reads=[sgn], writes=[sgn])
                    op("dve", lambda h: h.scalar_tensor_tensor(
                        out=vT[:, cc, :], in0=ba[:, 0:160], scalar=cols[:, C_BA + cc:C_BA + cc + 1], in1=sg[:],
                        op0=ALU.add, op1=ALU.mult), reads=["bank2", "cols", sgn], writes=["vT"])
                    if j == 0:
                        op("dve", lambda h: h.tensor_scalar(
                            out=vT[:, cc, 0:32], in0=vT[:, cc, 0:32], scalar1=cols[:, C_HALO:C_HALO + 1], scalar2=None,
                            op0=ALU.mult), reads=["vT", "cols"], writes=["vT"])

                def conv_taps(cc):
                    by = banks[2 + cc % 2]
                    byn = "bank%d" % (2 + cc % 2)
                    for jj in range(31):
                        op("pe", lambda h: h.matmul(
                            by[:, 0:128], lhsT=diag[:, cc, jj, :], rhs=vT[:, cc, 2 + jj:2 + jj + 128],
                            start=(jj == 0), stop=(jj == 30)), reads=["diag", "vT"], writes=[byn])
                    op("act", lambda h: h.activation(
                        out=yconv[:, cc, :], in_=by[:, 0:128], func=AF.Identity, bias=cols[:, C_DWB + cc:C_DWB + cc + 1]),
                        reads=[byn, "cols"], writes=["yconv"])
                    op("act", lambda h: h.activation(
                        out=ysq[:, cc, :], in_=by[:, 0:128], func=AF.Square, bias=cols[:, C_DWB + cc:C_DWB + cc + 1]),
                        reads=[byn, "cols"], writes=["ysq"])

                def conv_stats():
                    bm = banks[2]
                    bs = banks[3]
                    for cc in range(4):
                        op("pe", lambda h: h.matmul(bm[:, 0:128], lhsT=onesf[:], rhs=yconv[:, cc, :],
                                                    start=(cc == 0), stop=(cc == 3)), reads=["onesf", "yconv"], writes=["bank2"])
                    for cc in range(4):
                        op("pe", lambda h: h.matmul(bs[:, 0:128], lhsT=onesf[:], rhs=ysq[:, cc, :],
                                                    start=(cc == 0), stop=(cc == 3)), reads=["onesf", "ysq"], writes=["bank3"])
                    op("act", lambda h: h.activation(out=mean_sb[:], in_=bm[:, 0:128], func=AF.Copy), reads=["bank2"], writes=["mean_sb"])
                    op("dve", lambda h: h.tensor_tensor(out=var_sb[:], in0=mean_sb[:], in1=mean_sb[:], op=ALU.mult),
                       reads=["mean_sb"], writes=["var_sb"])
                    op("dve", lambda h: h.tensor_tensor(out=var_sb[:], in0=bs[:, 0:128], in1=var_sb[:], op=ALU.subtract),
                       reads=["bank3", "var_sb"], writes=["var_sb"])
                    op("dve", lambda h: h.tensor_scalar(out=var_sb[:], in0=var_sb[:], scalar1=0.0, scalar2=EPS, op0=ALU.max, op1=ALU.add),
                       reads=["var_sb"], writes=["var_sb"])

                def conv_ln_b():
                    op("act", lambda h: h.activation(out=rstd[:], in_=var_sb[:], func=AF.Ln), reads=["var_sb"], writes=["rstd"])
                    op("act", lambda h: h.activation(out=rstd[:], in_=rstd[:], func=AF.Exp, scale=-0.5), reads=["rstd"], writes=["rstd"])
                    op("dve", lambda h: h.tensor_tensor(out=mr[:], in0=mean_sb[:], in1=rstd[:], op=ALU.mult),
                       reads=["mean_sb", "rstd"], writes=["mr"])

                def conv_ln_c(cc):
                    tn = tnorm[cc % 2]
                    tnn = "tn%d" % (cc % 2)
                    op("dve", lambda h: h.tensor_tensor(out=tn[:], in0=yconv[:, cc, :], in1=rstd[:], op=ALU.mult),
                       reads=["yconv", "rstd"], writes=[tnn])
                    op("dve", lambda h: h.tensor_tensor(out=tn[:], in0=tn[:], in1=mr[:], op=ALU.subtract),
                       reads=[tnn, "mr"], writes=[tnn])

                def conv_ln_d(j, cc):
                    yt = yT[j % 2]
                    ytn = "yT%d" % (j % 2)
                    tn = tnorm[cc % 2]
                    tnn = "tn%d" % (cc % 2)
                    et = etn[cc % 2]
                    etnn = "etn%d" % (cc % 2)
                    op("act", lambda h: h.activation(
                        out=et[:], in_=tn[:], func=AF.Exp, bias=ncols[:, 12 + cc:13 + cc],
                        scale=ncols[:, 8 + cc:9 + cc]), reads=[tnn, "ncols"], writes=[etnn])
                    op("dve", lambda h: h.tensor_scalar(
                        out=tn[:], in0=tn[:], scalar1=cols[:, C_LNG + cc:C_LNG + cc + 1], scalar2=cols[:, C_LNB + cc:C_LNB + cc + 1],
                        op0=ALU.mult, op1=ALU.add), reads=[tnn, "cols"], writes=[tnn])
                    op("dve", lambda h: h.tensor_scalar(out=et[:], in0=et[:], scalar1=1.0, scalar2=None, op0=ALU.add),
                       reads=[etnn], writes=[etnn])
                    op("dve", lambda h: h.reciprocal(out=et[:], in_=et[:]), reads=[etnn], writes=[etnn])
                    op("dve", lambda h: h.tensor_tensor(out=yt[:, cc, :], in0=tn[:], in1=et[:], op=ALU.mult),
                       reads=[tnn, etnn], writes=[ytn])

                def hoisted_pieces(j):
                    x_ = xt[j % 2]
                    xn = "xt%d" % (j % 2)
                    gsig = gsigs[j % 2]
                    gsn = "gsig%d" % (j % 2)
                    yt = yT[j % 2]
                    ytn = "yT%d" % (j % 2)
                    abank = [(banks[2], "bank2"), (banks[7], "bank7")]

                    def s1_pe():
                        bq = banks[2]
                        for hh in range(4):
                            for k in range(8):
                                op("pe", lambda h: h.matmul(
                                    bq[:, hh * 128:(hh + 1) * 128], lhsT=wown[:, k, hh * 128:(hh + 1) * 128], rhs=x_[:, k, 32:160],
                                    start=(k == 0), stop=(k == 7)), reads=["wown", xn], writes=["bank2"])
                        bg = banks[3]
                        for k in range(8):
                            op("pe", lambda h: h.matmul(bg[:, 0:32], lhsT=x_[:, k, 32:160], rhs=wown[:, k, 1536:1568],
                                                        start=(k == 0), stop=(k == 7)), reads=["wown", xn], writes=["bank3"])

                    def s1_dve():
                        bq = banks[2]
                        bg = banks[3]
                        for g in range(2):
                            pr = slice(64 * g, 64 * g + 64)
                            op("dve", lambda h: h.tensor_tensor(
                                out=QTz4[j % 2][g][pr, :, :], in0=bq[pr, :].rearrange("p (a b) -> p a b", a=4),
                                in1=cols[pr, C_BQ:C_BQ + 4].unsqueeze(2).broadcast_to([64, 4, 128]), op=ALU.add),
                                reads=["bank2", "cols"], writes=["QTz%d_%d" % (j % 2, g)])
                        op("dve", lambda h: h.tensor_tensor(out=gsig[:], in0=bg[:, 0:32], in1=rows[:, R_BGT:R_BGT + 32], op=ALU.add),
                           reads=["bank3", "rows"], writes=[gsn])

                    def gate_act():
                        op("act", lambda h: h.activation(out=gsig[:], in_=gsig[:], func=AF.Exp, scale=-1.0), reads=[gsn], writes=[gsn])

                    def gate_fin():
                        op("dve", lambda h: h.tensor_scalar(out=gsig[:], in0=gsig[:], scalar1=1.0, scalar2=None, op0=ALU.add),
                           reads=[gsn], writes=[gsn])
                        op("dve", lambda h: h.reciprocal(out=gsig[:], in_=gsig[:]), reads=[gsn], writes=[gsn])

                    def cp_pe(cc):
                        ba, ban = abank[cc % 2]
                        bgp = banks[3]
                        for k in range(8):
                            op("pe", lambda h: h.matmul(
                                ba[:, 0:160], lhsT=wown[:, k, 512 + cc * 128:512 + (cc + 1) * 128], rhs=x_[:, k, :],
                                start=(k == 0), stop=(k == 7)), reads=["wown", xn], writes=[ban])
                        for k in range(8):
                            op("pe", lambda h: h.matmul(
                                bgp[:, 0:160], lhsT=wown[:, k, 1024 + cc * 128:1024 + (cc + 1) * 128], rhs=x_[:, k, :],
                                start=(k == 0), stop=(k == 7)), reads=["wown", xn], writes=["bank3"])

                    def cp_act(cc):
                        sg = sig[cc % 2]
                        op("act", lambda h: h.activation(
                            out=sg[:], in_=banks[3][:, 0:160], func=AF.Exp, bias=ncols[:, cc:cc + 1], scale=-1.0),
                            reads=["bank3", "ncols"], writes=["sig%d" % (cc % 2)])

                    def cp_dve(cc):
                        ba, ban = abank[cc % 2]
                        sg = sig[cc % 2]
                        sgn = "sig%d" % (cc % 2)
                        op("dve", lambda h: h.tensor_scalar(out=sg[:], in0=sg[:], scalar1=1.0, scalar2=None, op0=ALU.add),
                           reads=[sgn], writes=[sgn])
                        op("dve", lambda h: h.reciprocal(out=sg[:], in_=sg[:]), reads=[sgn], writes=[sgn])
                        op("dve", lambda h: h.scalar_tensor_tensor(
                            out=vT[:, cc, :], in0=ba[:, 0:160], scalar=cols[:, C_BA + cc:C_BA + cc + 1], in1=sg[:],
                            op0=ALU.add, op1=ALU.mult), reads=[ban, "cols", sgn], writes=["vT"])
                        if j == 0:
                            op("dve", lambda h: h.tensor_scalar(
                                out=vT[:, cc, 0:32], in0=vT[:, cc, 0:32], scalar1=cols[:, C_HALO:C_HALO + 1], scalar2=None,
                                op0=ALU.mult), reads=["vT", "cols"], writes=["vT"])

                    def tp_pe(cc):
                        by = banks[2 + cc % 2]
                        byn = "bank%d" % (2 + cc % 2)
                        for jj in range(31):
                            op("pe", lambda h: h.matmul(
                                by[:, 0:128], lhsT=diag[:, cc, jj, :], rhs=vT[:, cc, 2 + jj:2 + jj + 128],
                                start=(jj == 0), stop=(jj == 30)), reads=["diag", "vT"], writes=[byn])

                    def tp_dve(cc):
                        by = banks[2 + cc % 2]
                        byn = "bank%d" % (2 + cc % 2)
                        op("dve", lambda h: h.tensor_scalar(
                            out=yconv[:, cc, :], in0=by[:, 0:128], scalar1=cols[:, C_DWB + cc:C_DWB + cc + 1], scalar2=None, op0=ALU.add),
                            reads=[byn, "cols"], writes=["yconv"])
                        op("dve", lambda h: h.tensor_tensor(out=ysq[:, cc, :], in0=yconv[:, cc, :], in1=yconv[:, cc, :], op=ALU.mult),
                           reads=["yconv"], writes=["ysq"])

                    def st_pe():
                        for cc in range(4):
                            op("pe", lambda h: h.matmul(banks[2][:, 0:128], lhsT=onesf[:], rhs=yconv[:, cc, :],
                                                        start=(cc == 0), stop=(cc == 3)), reads=["onesf", "yconv"], writes=["bank2"])
                        for cc in range(4):
                            op("pe", lambda h: h.matmul(banks[3][:, 0:128], lhsT=onesf[:], rhs=ysq[:, cc, :],
                                                        start=(cc == 0), stop=(cc == 3)), reads=["onesf", "ysq"], writes=["bank3"])

                    def st_dve():
                        op("dve", lambda h: h.tensor_copy(out=mean_sb[:], in_=banks[2][:, 0:128]), reads=["bank2"], writes=["mean_sb"])
                        op("dve", lambda h: h.tensor_tensor(out=var_sb[:], in0=mean_sb[:], in1=mean_sb[:], op=ALU.mult),
                           reads=["mean_sb"], writes=["var_sb"])
                        op("dve", lambda h: h.tensor_tensor(out=var_sb[:], in0=banks[3][:, 0:128], in1=var_sb[:], op=ALU.subtract),
                           reads=["bank3", "var_sb"], writes=["var_sb"])
                        op("dve", lambda h: h.tensor_scalar(out=var_sb[:], in0=var_sb[:], scalar1=0.0, scalar2=EPS, op0=ALU.max, op1=ALU.add),
                           reads=["var_sb"], writes=["var_sb"])

                    def rs_act():
                        op("act", lambda h: h.activation(out=rstd[:], in_=var_sb[:], func=AF.Ln), reads=["var_sb"], writes=["rstd"])
                        op("act", lambda h: h.activation(out=rstd[:], in_=rstd[:], func=AF.Exp, scale=-0.5), reads=["rstd"], writes=["rstd"])

                    def mr_dve():
                        op("dve", lambda h: h.tensor_tensor(out=mr[:], in0=mean_sb[:], in1=rstd[:], op=ALU.mult),
                           reads=["mean_sb", "rstd"], writes=["mr"])

                    def sl_act(cc):
                        tn = tnorm[cc % 2]
                        et = etn[cc % 2]
                        op("act", lambda h: h.activation(
                            out=et[:], in_=tn[:], func=AF.Exp, bias=ncols[:, 12 + cc:13 + cc],
                            scale=ncols[:, 8 + cc:9 + cc]), reads=["tn%d" % (cc % 2), "ncols"], writes=["etn%d" % (cc % 2)])

                    def sl_dve(cc):
                        tn = tnorm[cc % 2]
                        tnn = "tn%d" % (cc % 2)
                        et = etn[cc % 2]
                        etnn = "etn%d" % (cc % 2)
                        op("dve", lambda h: h.tensor_scalar(
                            out=tn[:], in0=tn[:], scalar1=cols[:, C_LNG + cc:C_LNG + cc + 1], scalar2=cols[:, C_LNB + cc:C_LNB + cc + 1],
                            op0=ALU.mult, op1=ALU.add), reads=[tnn, "cols"], writes=[tnn])
                        op("dve", lambda h: h.tensor_scalar(out=et[:], in0=et[:], scalar1=1.0, scalar2=None, op0=ALU.add),
                           reads=[etnn], writes=[etnn])
                        op("dve", lambda h: h.reciprocal(out=et[:], in_=et[:]), reads=[etnn], writes=[etnn])
                        op("dve", lambda h: h.tensor_tensor(out=yt[:, cc, :], in0=tn[:], in1=et[:], op=ALU.mult),
                           reads=[tnn, etnn], writes=[ytn])

                    seq = [
                        [s1_pe],
                        [s1_dve, lambda: cp_pe(0)],
                        [gate_act, lambda: cp_act(0)],
                        [gate_fin, lambda: cp_dve(0), lambda: cp_pe(1)],
                        [lambda: cp_act(1)],
                        [lambda: cp_dve(1), lambda: cp_pe(2)],
                        [lambda: cp_act(2)],
                        [lambda: cp_dve(2), lambda: cp_pe(3)],
                        [lambda: cp_act(3)],
                        [lambda: cp_dve(3), lambda: tp_pe(0)],
                        [lambda: tp_dve(0), lambda: tp_pe(1)],
                        [lambda: tp_dve(1), lambda: tp_pe(2)],
                        [lambda: tp_dve(2), lambda: tp_pe(3)],
                        [lambda: tp_dve(3)],
                        [st_pe],
                        [st_dve],
                        [rs_act],
                        [mr_dve, lambda: conv_ln_c(0)],
                        [lambda: sl_act(0), lambda: conv_ln_c(1)],
                        [lambda: sl_dve(0), lambda: sl_act(1), lambda: conv_ln_c(2)],
                        [lambda: sl_dve(1), lambda: sl_act(2), lambda: conv_ln_c(3)],
                        [lambda: sl_dve(2), lambda: sl_act(3)],
                        [lambda: sl_dve(3)],
                    ]
                    return [(lambda fs=fs: [f() for f in fs]) for fs in seq]

                for f in hoisted_pieces(0):
                    f()
                for i in range(NSTEP):
                    bo_ = bon[i % 2]
                    bon_n = "bon%d" % (i % 2)
                    yt = yT[i % 2]
                    ytn = "yT%d" % (i % 2)
                    gsig = gsigs[i % 2]
                    gsn = "gsig%d" % (i % 2)
                    qp = i % 2
                    if i + 2 < NSTEP:
                        load_x(i + 2)
                    if i + 1 < NSTEP:
                        load_bon(i + 1)
                    if i < 8:
                        dma("pool", "d6", lambda h: h.dma_start(out=wupb[:, i:i + 1, :], in_=w_up[:, i:i + 1, :]), writes=["wupb"])
                    elif i < 16:
                        j8 = i - 8
                        dma("pool", "d6", lambda h: h.dma_start(out=wdnb[:, 4 * j8:4 * j8 + 4, :], in_=w_dn[:, 4 * j8:4 * j8 + 4, :]),
                            writes=["wdnb"])

                    NTc = (16 * i + 14) // 128 + 1
                    cmp_fin = []
                    units = []
                    for g in range(2):
                        pc = PcT[g]
                        pcn = "PcT%d" % g
                        ob = banks[4 + g]
                        obn = "bank%d" % (4 + g)
                        for nt in range(NTc):
                            masks = []
                            if nt >= NTc - 2:
                                which = 1 if nt == NTc - 1 else 0
                                masks.append((i30k[:], cmask[:, i % 8, which, :], "cmask"))
                            units.append(mk_unit(kct[:, nt * 128:(nt + 1) * 128], "kct", g, masks,
                                                 vct[:, nt, g, :], "vct", ob, obn, nt == 0, nt == NTc - 1,
                                                 pdst=pc[:, nt, :], pdn=pcn, qpar=qp))
                        cmp_fin.append((ob, obn, pc, pcn))
                    if tail[0] is not None:
                        for pi, pf in enumerate(tail[0]):
                            units[min(pi, len(units) - 1)]["after"].append((0, pf))
                        tail[0] = None
                    run_units(units)
                    ubank = [(banks[7], "bank7"), (banks[2], "bank2")]
                    for g in range(2):
                        ob, obn, pc, pcn = cmp_fin[g]
                        osb, osn = fin1(ob, obn)
                        cmp_fin[g] = (osb, osn, pc, pcn)
                        bu, bun = ubank[g]
                        for hh in range(4):
                            for nt in range(NTc):
                                op("pe", lambda h: h.matmul(
                                    bu[:, hh * 128:(hh + 1) * 128], lhsT=pc[:, nt, hh * 128:(hh + 1) * 128], rhs=mov[:, nt, :],
                                    start=(nt == 0), stop=(nt == NTc - 1)), reads=[pcn, "mov"], writes=[bun])

                    def cmp_epi(g):
                        osb, osn, pc, pcn = cmp_fin[g]
                        fin2(osb, osn, g, 0, gsig, gsn)

                    def cmp_topk(g):
                        topk_chain(g, bo_, bon_n, ubank[g][0], ubank[g][1])

                    def cmp_post(g):
                        cmp_epi(g)
                        cmp_topk(g)

                    units = []
                    for g in range(2):
                        ob = banks[4 + g]
                        obn = "bank%d" % (4 + g)
                        kts = [kt for kt in range(2 * i - 4, 2 * i + 2) if kt >= 0]
                        for idx, kt in enumerate(kts):
                            sl = kt - (2 * i - 4)
                            masks = [] if sl in (2, 3) else [(i30k[:], wmask[:, sl, :], "wmask")]
                            units.append(mk_unit(kwT[:, kt * 128:(kt + 1) * 128], "kwT", g, masks,
                                                 vsw[:, kt, 2 + g, :], "vsw", ob, obn, idx == 0, idx == len(kts) - 1, qpar=qp))

                        def fw(ob=ob, obn=obn, g=g, units=units, at=len(units) - 1, gsig=gsig, gsn=gsn):
                            osb, osn = fin1(ob, obn)
                            units[at]["after"].append((2, lambda: fin2(osb, osn, g, 2, gsig, gsn)))
                        units[-1]["after"].append((0, fw))
                    nk = 2 * i + 2
                    defer_g1 = nk >= 24
                    units[0]["after"].insert(0, (0, lambda: cmp_post(0)))
                    if not defer_g1:
                        units[min(3, len(units) - 1)]["after"].insert(0, (0, lambda: cmp_post(1)))
                    else:
                        units[3]["after"].insert(0, (0, lambda: cmp_epi(1)))
                    run_units(units)

                    def neg_pe(g):
                        bt = banks[7]
                        op("pe", lambda h: h.matmul(bt[:, 256 + g * 128:256 + (g + 1) * 128], lhsT=negs[g][:], rhs=identb[:],
                                                    start=True, stop=True), reads=["neg%d" % g, "identb"], writes=["bank7"])

                    def neg_cp(g):
                        bt = banks[7]
                        op("dve", lambda h: h.tensor_copy(out=negT[g][:], in_=bt[:, 256 + g * 128:256 + (g + 1) * 128]),
                           reads=["bank7"], writes=["negT%d" % g])
                    neg_pe(0)
                    neg_cp(0)
                    if not defer_g1:
                        neg_pe(1)
                        neg_cp(1)

                    units = []
                    last_fin = {}
                    for g in range(2):
                        ob = banks[4 + g]
                        obn = "bank%d" % (4 + g)
                        for kt in range(nk):
                            masks = [(b30k[:, kt * 128:(kt + 1) * 128], negT[g][:], "negT%d" % g)]
                            if kt >= 2 * i:
                                masks.append((i30k[:], dmask[:, kt - 2 * i, :], "dmask"))
                            units.append(mk_unit(ksT[:, kt * 128:(kt + 1) * 128], "ksT", g, masks,
                                                 vsw[:, kt, g, :], "vsw", ob, obn, kt == 0, kt == nk - 1, qpar=qp))
                        if g == 0:
                            def fs(ob=ob, obn=obn, units=units, at=len(units) - 1, gsig=gsig, gsn=gsn):
                                osb, osn = fin1(ob, obn)
                                units[at]["after"].append((2, lambda: fin2(osb, osn, 0, 1, gsig, gsn)))
                            units[-1]["after"].append((0, fs))
                        else:
                            def fs1(ob=ob, obn=obn):
                                last_fin["o"] = fin1(ob, obn)
                            units[-1]["after"].append((0, fs1))
                    if i + 1 < NSTEP:
                        hp = hoisted_pieces(i + 1)
                        by_unit = {}
                        off = 14 if defer_g1 else 1
                        sp_ = 2 if len(units) >= 2 * len(hp) + off + 1 else 1
                        for pi, pf in enumerate(hp):
                            by_unit.setdefault(min(off + sp_ * pi, len(units) - 1), []).append((0, pf))
                        for at, lst in by_unit.items():
                            units[at]["after"] = lst + units[at]["after"]
                    if defer_g1:
                        units[0]["after"].insert(0, (0, lambda: cmp_topk(1)))
                        units[12]["after"].insert(0, (0, lambda: neg_pe(1)))
                        units[13]["after"].insert(0, (0, lambda: neg_cp(1)))
                    run_units(units)

                    def mk_tail(i=i, yt=yt, ytn=ytn, lf=last_fin, gsig=gsig, gsn=gsn):
                        def ta():
                            osb, osn = lf["o"]
                            fin2(osb, osn, 1, 1, gsig, gsn)

                        def tb():
                            op("dve", lambda h: h.tensor_copy(out=ynb[:], in_=yacc[:].rearrange("p a b -> p (a b)")),
                               reads=["yacc"], writes=["ynb"])

                        def tc():
                            bt = banks[7]
                            for fc in range(4):
                                op("pe", lambda h: h.matmul(bt[:, fc * 128:(fc + 1) * 128], lhsT=ynb[:, fc * 128:(fc + 1) * 128],
                                                            rhs=identb[:], start=True, stop=True),
                                   reads=["ynb", "identb"], writes=["bank7"])

                        def td():
                            bt = banks[7]
                            op("dve", lambda h: h.tensor_copy(out=yt[:, 4:8, :], in_=bt[:].rearrange("p (a b) -> p a b", a=4)),
                               reads=["bank7"], writes=[ytn])
                            dma("sp", "d2", lambda h: h.dma_start(out=ystash[i], in_=yt[:]), reads=[ytn], writes=["ystash%d" % (i % 2)])
                        return [ta, tb, tc, td]
                    tail[0] = mk_tail()
                for pf in tail[0]:
                    pf()
        P.barrier()

        with contextlib.ExitStack() as p2:
            wout = sbt(p2, "wout", [128, 8, D], BF16)
            wpg = sbt(p2, "wpg", [128, 8, D], BF16)
            wpe = sbt(p2, "wpe", [128, 2, D], BF16)
            wup = [sbt(p2, "wup0", [128, 8, 512], BF16), sbt(p2, "wup1", [128, 8, 512], BF16)]
            wdn = [sbt(p2, "wdn0", [128, 4, 512], BF16), sbt(p2, "wdn1", [128, 4, 512], BF16)]
            yTb = [sbt(p2, "yTb0", [128, 4, 8, 128], BF16)] * 2
            pTb = [sbt(p2, "pTb0", [128, 4, 2, 128], BF16)] * 2
            xin = [sbt(p2, "xin0", [128, D], F32), sbt(p2, "xin1", [128, D], F32)]
            x1f = [sbt(p2, "x1f%d" % k, [128, 4, D], F32) for k in range(2)]
            x1b = [sbt(p2, "x1b0", [128, D], BF16), sbt(p2, "x1b1", [128, D], BF16)]
            x1T = [sbt(p2, "x1T%d" % k, [128, 8, 4, 128], BF16) for k in range(2)]
            hT = sbt(p2, "hT", [128, 32, 512], BF16)
            htmp = [sbt(p2, "htmp0", [128, 512], F32), sbt(p2, "htmp1", [128, 512], F32)]
            r1s = [sbt(p2, "r1_0", [128, D], F32), sbt(p2, "r1_1", [128, D], F32)]
            sgp = sbt(p2, "sgp", [128, 512], F32)
            stats = sbt(p2, "stats", [128, 8, 2, 6], F32)
            mv = sbt(p2, "mv", [128, 8, 2], F32)
            rs = sbt(p2, "rs", [128, 8], F32)
            obuf = [sbt(p2, "obuf0", [128, D], F32), sbt(p2, "obuf1", [128, D], F32)]

            dma("pool", "d0", lambda h: h.dma_start(out=wout[:], in_=w_out), writes=["wout"])
            dma("pool", "d0", lambda h: h.dma_start(out=wpg[:], in_=w_pg), writes=["wpg"])
            dma("pool", "d0", lambda h: h.dma_start(out=wpe[:], in_=w_pe), writes=["wpe"])

            def ln_a(src, srcn, k):
                for hf in range(2):
                    op("dve", lambda h: h.bn_stats(out=stats[:, k, hf, :], in_=src[:, hf * 512:(hf + 1) * 512]),
                       reads=[srcn], writes=["stats%d" % k])
                op("dve", lambda h: h.bn_aggr(out=mv[:, k, :], in_=stats[:, k, :, :].rearrange("p a b -> p (a b)")),
                   reads=["stats%d" % k], writes=["mv%d" % k])
                op("dve", lambda h: h.tensor_scalar(out=rs[:, k:k + 1], in0=mv[:, k, 1:2], scalar1=0.0, scalar2=EPS, op0=ALU.max, op1=ALU.add),
                   reads=["mv%d" % k], writes=["rs%d" % k])

            def ln_b(src, srcn, dst, dstn, k, rg, rb):
                op("act", lambda h: h.activation(out=rs[:, k:k + 1], in_=rs[:, k:k + 1], func=AF.Ln), reads=["rs%d" % k], writes=["rs%d" % k])
                op("act", lambda h: h.activation(out=rs[:, k:k + 1], in_=rs[:, k:k + 1], func=AF.Exp, scale=-0.5),
                   reads=["rs%d" % k], writes=["rs%d" % k])
                op("dve", lambda h: h.scalar_tensor_tensor(out=dst, in0=src[:], scalar=mv[:, k, 0:1], in1=rows[:, rg:rg + D],
                                                           op0=ALU.subtract, op1=ALU.mult),
                   reads=[srcn, "mv%d" % k, "rows"], writes=[dstn])
                op("dve", lambda h: h.scalar_tensor_tensor(out=dst, in0=dst, scalar=rs[:, k:k + 1], in1=rows[:, rb:rb + D],
                                                           op0=ALU.mult, op1=ALU.add),
                   reads=[dstn, "rs%d" % k, "rows"], writes=[dstn])

            deferred = []

            def pre_pieces(blk):
                par = blk % 2
                yb, pb, xf, xT = yTb[par], pTb[par], x1f[par], x1T[par]
                ybn, pbn, xfn, xTn = "yTb0", "pTb0", "x1f%d" % par, "x1T%d" % par

                def loads():
                    for tl in range(4):
                        ti = blk * 4 + tl
                        dma("sp", "d1", lambda h: h.dma_start(out=yb[:, tl, :, :], in_=ystash[ti]),
                            reads=["ystash0", "ystash1"], writes=[ybn + "_%d" % tl])
                        dma("pool", "d3", lambda h: h.dma_start(out=pb[:, tl, :, :], in_=pTo[ti]), writes=[pbn + "_%d" % tl])

                def outproj(tl, hf):
                    ti = blk * 4 + tl
                    xi = xin[tl % 2]
                    r1_ = r1s[tl % 2]
                    if hf == 0:
                        dma("pool", "d1", lambda h: h.dma_start(out=xi[:], in_=xo[ti]), writes=["xin%d" % (tl % 2)])
                    bk = banks[2 + hf]
                    for kc in range(8):
                        op("pe", lambda h: h.matmul(
                            bk[:], lhsT=yb[:, tl, kc, :], rhs=wout[:, kc, hf * 512:(hf + 1) * 512],
                            start=(kc == 0), stop=(kc == 7)), reads=[ybn + "_%d" % tl, "wout"], writes=["bank%d" % (2 + hf)])
                    op("dve", lambda h: h.tensor_tensor(
                        out=r1_[:, hf * 512:(hf + 1) * 512], in0=bk[:], in1=rows[:, R_BOUT + hf * 512:R_BOUT + (hf + 1) * 512],
                        op=ALU.add), reads=["bank%d" % (2 + hf), "rows"], writes=["r1_%d" % (tl % 2)])

                def lnA(tl):
                    xi = xin[tl % 2]
                    r1_ = r1s[tl % 2]
                    op("dve", lambda h: h.scalar_tensor_tensor(out=r1_[:], in0=xi[:], scalar=ALPHA, in1=r1_[:],
                                                               op0=ALU.mult, op1=ALU.add),
                       reads=["xin%d" % (tl % 2), "r1_%d" % (tl % 2)], writes=["r1_%d" % (tl % 2)])
                    ln_a(r1_, "r1_%d" % (tl % 2), tl)

                def lnB(tl):
                    r1_ = r1s[tl % 2]
                    xb_ = x1b[tl % 2]
                    ln_b(r1_, "r1_%d" % (tl % 2), xf[:, tl, :], xfn, tl, R_G1, R_B1)
                    op("pool", lambda h: h.tensor_copy(out=xb_[:], in_=xf[:, tl, :]), reads=[xfn], writes=["x1b%d" % (tl % 2)])

                def xpose(tl):
                    xb_ = x1b[tl % 2]
                    for hf in range(2):
                        bk = banks[2 + hf]
                        for q4 in range(4):
                            kc = hf * 4 + q4
                            op("pe", lambda h: h.matmul(
                                bk[:, q4 * 128:(q4 + 1) * 128], lhsT=xb_[:, kc * 128:(kc + 1) * 128], rhs=identb[:],
                                start=True, stop=True), reads=["x1b%d" % (tl % 2), "identb"], writes=["bank%d" % (2 + hf)])
                        op("dve", lambda h: h.tensor_copy(
                            out=xT[:, hf * 4:hf * 4 + 4, tl, :], in_=bk[:].rearrange("p (a b) -> p a b", a=4)),
                            reads=["bank%d" % (2 + hf)], writes=[xTn])

                def ple(tl, hf):
                    ple_pe(tl, hf)
                    deferred.append(lambda: ple_post(tl, hf))

                def ple_pe(tl, hf):
                    bpe = banks[2]
                    bpg = banks[3]
                    for kc in range(2):
                        op("pe", lambda h: h.matmul(
                            bpe[:], lhsT=pb[:, tl, kc, :], rhs=wpe[:, kc, hf * 512:(hf + 1) * 512],
                            start=(kc == 0), stop=(kc == 1)), reads=[pbn + "_%d" % tl, "wpe"], writes=["bank2"])
                    for kc in range(8):
                        op("pe", lambda h: h.matmul(
                            bpg[:], lhsT=xT[:, kc, tl, :], rhs=wpg[:, kc, hf * 512:(hf + 1) * 512],
                            start=(kc == 0), stop=(kc == 7)), reads=[xTn, "wpg"], writes=["bank3"])

                def ple_post(tl, hf):
                    bpe = banks[2]
                    bpg = banks[3]
                    op("act", lambda h: h.activation(out=sgp[:], in_=bpg[:], func=AF.Exp, scale=-1.0), reads=["bank3"], writes=["sgp"])
                    op("dve", lambda h: h.tensor_scalar(out=sgp[:], in0=sgp[:], scalar1=1.0, scalar2=None, op0=ALU.add),
                       reads=["sgp"], writes=["sgp"])
                    op("dve", lambda h: h.reciprocal(out=sgp[:], in_=sgp[:]), reads=["sgp"], writes=["sgp"])
                    op("dve", lambda h: h.tensor_tensor(out=sgp[:], in0=bpe[:], in1=sgp[:], op=ALU.mult),
                       reads=["bank2", "sgp"], writes=["sgp"])
                    xs = xf[:, tl, hf * 512:(hf + 1) * 512]
                    op("dve", lambda h: h.scalar_tensor_tensor(out=xs, in0=xs, scalar=ALPHA, in1=sgp[:],
                                                               op0=ALU.mult, op1=ALU.add),
                       reads=[xfn, "sgp"], writes=[xfn])
                    op("dve", lambda h: h.tensor_tensor(
                        out=xs, in0=xs, in1=rows[:, R_BDN + hf * 512:R_BDN + (hf + 1) * 512], op=ALU.add),
                        reads=[xfn, "rows"], writes=[xfn])

                O, A, B, X, Pl = outproj, lnA, lnB, xpose, ple
                order = [(loads,), (O, 0, 0), (O, 0, 1), (A, 0), (O, 1, 0), (O, 1, 1), (B, 0), (A, 1), (X, 0),
                         (O, 2, 0), (O, 2, 1), (B, 1), (Pl, 0, 0), (A, 2), (X, 1), (Pl, 0, 1), (O, 3, 0), (O, 3, 1),
                         (B, 2), (Pl, 1, 0), (A, 3), (X, 2), (Pl, 1, 1), (B, 3), (Pl, 2, 0), (X, 3), (Pl, 2, 1),
                         (Pl, 3, 0), (Pl, 3, 1)]
                def run_piece(t):
                    while deferred:
                        deferred.pop(0)()
                    t[0](*t[1:])
                return [(lambda t=t: run_piece(t)) for t in order] + [lambda: run_piece((lambda: None,))]

            def post_pieces(blk):
                par = blk % 2
                xf = x1f[par]
                xfn = "x1f%d" % par

                def pa(tl):
                    ln_a(xf[:, tl, :], xfn, 4 + tl)

                def pb_(tl):
                    ti = blk * 4 + tl
                    ob_ = obuf[ti % 2]
                    obn_ = "obuf%d" % (ti % 2)
                    ln_b(xf[:, tl, :], xfn, ob_[:], obn_, 4 + tl, R_G2, R_B2)
                    dma("pool", "d2", lambda h: h.dma_start(out=out_d[ti], in_=ob_[:]), reads=[obn_], writes=["out%d" % (ti % 2)])
                order = [(pa, 0), (pa, 1), (pb_, 0), (pa, 2), (pb_, 1), (pa, 3), (pb_, 2), (pb_, 3)]
                return [(lambda t=t: t[0](*t[1:])) for t in order]

            upctr = [0]
            dnctr = [0]
            for f in pre_pieces(0):
                f()
            pending_post = []
            for blk in range(8):
                par = blk % 2
                xf, xT = x1f[par], x1T[par]
                xfn, xTn = "x1f%d" % par, "x1T%d" % par
                nxt = pending_post + (pre_pieces(blk + 1) if blk + 1 < 8 else [])
                nxt.reverse()

                def filler(k=1):
                    for _ in range(k):
                        if nxt:
                            nxt.pop()()
                for c4 in range(8):
                    u = upctr[0]
                    upctr[0] += 1
                    wu = wup[u % 2]
                    wun = "wup%d" % (u % 2)
                    dma("sp", "d4", lambda h: h.dma_start(out=wu[:], in_=wupb[:, :, c4 * 512:(c4 + 1) * 512]),
                        reads=["wupb"], writes=[wun])
                    for cs in range(4):
                        c = c4 * 4 + cs
                        bk = banks[c % 2]
                        bn = "bank%d" % (c % 2)
                        for kc in range(8):
                            op("pe", lambda h: h.matmul(
                                bk[:], lhsT=wu[:, kc, cs * 128:(cs + 1) * 128], rhs=xT[:, kc, :, :].rearrange("p a b -> p (a b)"),
                                start=(kc == 0), stop=(kc == 7)), reads=[wun, xTn], writes=[bn])
                        ht = htmp[c % 2]
                        htn = "htmp%d" % (c % 2)
                        op("act", lambda h: h.activation(
                            out=ht[:], in_=bk[:], func=AF.Relu, bias=cols[:, C_BUP + c:C_BUP + c + 1]),
                            reads=[bn, "cols"], writes=[htn])
                        op("act", lambda h: h.activation(out=hT[:, c, :], in_=ht[:], func=AF.Square),
                           reads=[htn], writes=["hT"])
                        if c % 4 != 3:
                            filler(1)
                for hf in range(2):
                    for c4 in range(8):
                        u = dnctr[0]
                        dnctr[0] += 1
                        wd = wdn[u % 2]
                        wdn_n = "wdn%d" % (u % 2)
                        dma("sp", "d4", lambda h: h.dma_start(
                            out=wd[:], in_=wdnb[:, c4 * 4:(c4 + 1) * 4, hf * 512:(hf + 1) * 512]), reads=["wdnb"], writes=[wdn_n])
                        for cs in range(4):
                            c = c4 * 4 + cs
                            for tl in range(4):
                                op("pe", lambda h: h.matmul(
                                    banks[4 + tl][:], lhsT=hT[:, c, tl * 128:(tl + 1) * 128], rhs=wd[:, cs, :],
                                    start=(c == 0), stop=(c == 31)), reads=["hT", wdn_n], writes=["bank%d" % (4 + tl)])
                        filler(1)
                    for tl in range(4):
                        xs = xf[:, tl, hf * 512:(hf + 1) * 512]
                        op("dve", lambda h: h.tensor_tensor(out=xs, in0=banks[4 + tl][:], in1=xs, op=ALU.add),
                           reads=["bank%d" % (4 + tl), xfn], writes=[xfn])
                filler(100)
                pending_post = post_pieces(blk)
            for f in pending_post:
                f()
        P.final_wait("sp")
        P.replay()
    P.close()
    return nc


def _chunk_rows(w):
    K, C = w.shape
    return np.ascontiguousarray(w.reshape(K // 128, 128, C).transpose(1, 0, 2))


def _const_tables(h):
    f32 = np.float32
    kl = np.arange(128)[:, None]
    ql = np.arange(128)[None, :]
    wm = np.zeros((128, 6, 128), f32)
    for s in range(6):
        dlt = h + 4 - s
        if dlt < 0 or dlt == 5:
            wm[:, s, :] = -1.0
        elif dlt == 0:
            wm[:, s, :] = np.where(kl > ql, -1.0, 0.0)
        elif dlt == 4:
            wm[:, s, :] = np.where(kl <= ql, -1.0, 0.0)
    dm = np.zeros((128, 2, 128), f32)
    tri = np.where(kl > ql, -1.0, 0.0)
    if h == 0:
        dm[:, 0, :] = tri
        dm[:, 1, :] = -1.0
    else:
        dm[:, 1, :] = tri
    cm = np.zeros((128, 8, 2, 128), f32)
    for m in range(8):
        r = 2 * m + h
        lhs = 16 * kl + 31 - ql
        cm[:, m, 1, :] = np.where(lhs <= 128 * r, 0.0, -1.0)
        cm[:, m, 0, :] = np.where(lhs <= 128 * r + 2048, 0.0, -1.0)
    bonus = np.zeros((NSTEP, 128, 128), f32)
    j = np.arange(128)[None, :]
    for i in range(NSTEP):
        pos = 128 * (2 * i + h) + np.arange(128)[:, None]
        cur = pos // 64
        valid = 64 * j <= pos
        forced = (j == 0) | (j == cur) | (j == cur - 1)
        bonus[i] = np.where(valid, np.where(forced, 1e4, 0.0), -2e30)
    n = np.arange(512)[:, None]
    jj = np.arange(128)[None, :]
    mo = ((16 * n <= 64 * jj + 63) & (16 * n + 31 >= 64 * jj) & (n < 511)).astype(f32)
    mov = np.ascontiguousarray(mo.reshape(4, 128, 128).transpose(1, 0, 2))
    return wm, dm, cm, bonus, mov


def _prep_shared(inp):
    f32 = np.float32
    w_in = np.asarray(inp["w_in"][0], f32)
    b_in = np.asarray(inp["b_in"][0], f32)
    qcols = []
    for hh in range(4):
        for g in range(2):
            qcols += list(1024 + g * 256 + hh * 64 + np.arange(64))
    gcols = []
    for g in range(2):
        for br in range(3):
            for hh in range(4):
                gcols.append(2304 + g * 12 + hh * 3 + br)
    own = np.zeros((1024, 1568), f32)
    own[:, 0:512] = w_in[:, qcols]
    own[:, 512:1536] = w_in[:, 0:1024]
    own[:, 1536:1560] = w_in[:, gcols]
    kvcols = np.concatenate([np.arange(1536, 1664), np.arange(1664, 1792), np.arange(1792, 1920),
                             np.arange(2048, 2176), np.arange(1920, 2048), np.arange(2176, 2304)])
    sh = {}
    sh["w_own"] = _chunk_rows(own)
    sh["w_kv"] = _chunk_rows(w_in[:, kvcols])
    for nm, t in (("k", "cmp_w1_k"), ("v", "cmp_w1_v")):
        w1 = np.asarray(inp[t][0], f32)
        sh["w1" + nm] = np.ascontiguousarray(w1.reshape(32, 64, 256).transpose(1, 0, 2))
    w2k = np.asarray(inp["cmp_w2_k"][0], f32)
    w2kp = np.zeros((128, 2, 2, 128), f32)
    for g in range(2):
        for hc in range(2):
            w2kp[:, g, hc, 64 * g:64 * g + 64] = w2k[hc * 128:(hc + 1) * 128, :]
    sh["w2k"] = w2kp
    sh["w2v"] = _chunk_rows(np.asarray(inp["cmp_w2_v"][0], f32))
    sh["posk"] = np.ascontiguousarray(np.asarray(inp["cmp_pos_k"][0], f32).T)
    sh["posv"] = np.ascontiguousarray(np.asarray(inp["cmp_pos_v"][0], f32).T)
    sh["w_out"] = _chunk_rows(np.asarray(inp["w_out"][0], f32))
    sh["w_pg"] = _chunk_rows(np.asarray(inp["w_pg"][0], f32))
    sh["w_pe"] = _chunk_rows(np.asarray(inp["w_pe"][0], f32))
    sh["w_up"] = _chunk_rows(np.asarray(inp["w_up"][0], f32))
    sh["w_dn"] = _chunk_rows(np.asarray(inp["w_down"][0], f32))
    sh["ident"] = np.eye(128, dtype=f32)
    cols = np.zeros((128, NCOL), f32)
    bq = b_in[qcols]
    for hh in range(4):
        cols[:, C_BQ + hh] = bq[hh * 128:(hh + 1) * 128]
    bkv = b_in[kvcols]
    for cb in range(4):
        cols[:, C_BKV + cb] = bkv[cb * 128:(cb + 1) * 128]
    dww = np.asarray(inp["conv_dw_w"][0], f32)[:, 0, :]
    for cc in range(4):
        sl = slice(cc * 128, (cc + 1) * 128)
        cols[:, C_BA + cc] = b_in[0:512][sl]
        cols[:, C_BG + cc] = b_in[512:1024][sl]
        cols[:, C_DWB + cc] = np.asarray(inp["conv_dw_b"][0], f32)[sl]
        cols[:, C_LNG + cc] = np.asarray(inp["conv_ln_g"][0], f32)[sl]
        cols[:, C_LNB + cc] = np.asarray(inp["conv_ln_b"][0], f32)[sl]
        cols[:, C_DW + cc * 31:C_DW + (cc + 1) * 31] = dww[:, sl].T
    cols[:, C_BUP:C_BUP + 32] = np.asarray(inp["b_up"][0], f32).reshape(32, 128).T
    sh["cols_base"] = cols
    rows = np.zeros((NROW,), f32)
    rows[R_BV:R_BV + 256] = bkv[512:768]
    rows[R_BGT:R_BGT + 24] = b_in[gcols]
    rows[R_BOUT:R_BOUT + D] = inp["b_out"][0]
    rows[R_G1:R_G1 + D] = inp["ln1_g"][0]
    rows[R_B1:R_B1 + D] = inp["ln1_b"][0]
    rows[R_BDN:R_BDN + D] = inp["b_down"][0]
    rows[R_G2:R_G2 + D] = inp["ln2_g"][0]
    rows[R_B2:R_B2 + D] = inp["ln2_b"][0]
    sh["rows"] = np.ascontiguousarray(np.broadcast_to(rows[None, :], (128, NROW)))
    return sh


def _prep_core(inp, sh, c, consts):
    f32 = np.float32
    b, h = c // 2, c % 2
    x = np.asarray(inp["x"][b], f32)
    p = np.asarray(inp["p"][0, b], f32)
    m = {k: v for k, v in sh.items() if k != "cols_base"}
    xTfull = np.ascontiguousarray(x.T)
    m["xT"] = np.ascontiguousarray(xTfull.reshape(8, 128, S).transpose(1, 0, 2))
    xpad = np.concatenate([np.zeros((32, D), f32), x], axis=0)
    xto = np.zeros((NSTEP, 128, 8, 160), f32)
    xo = np.zeros((NSTEP, 128, D), f32)
    pto = np.zeros((NSTEP, 128, 2, 128), f32)
    for i in range(NSTEP):
        qt = 2 * i + h
        seg = xpad[128 * qt:128 * qt + 160]
        xto[i] = seg.T.reshape(8, 128, 160).transpose(1, 0, 2)
        xo[i] = x[128 * qt:128 * qt + 128]
        pto[i] = p[128 * qt:128 * qt + 128].T.reshape(2, 128, 128).transpose(1, 0, 2)
    m["xTo"], m["xo"], m["pTo"] = xto, xo, pto
    cols = sh["cols_base"].copy()
    cols[:, C_HALO] = float(h)
    m["cols"] = cols
    wm, dm, cm, bonus, mov = consts[h]
    m["wmask"], m["dmask"], m["cmask"], m["bonus"], m["mov"] = wm, dm, cm, bonus, mov
    return m


_CACHE = {}


def kernel(**inputs):
    if "nc" not in _CACHE:
        _CACHE["nc"] = build_program()
        _CACHE["consts"] = {h: _const_tables(h) for h in range(2)}
    nc = _CACHE["nc"]
    sh = _prep_shared(inputs)
    in_maps = [_prep_core(inputs, sh, c, _CACHE["consts"]) for c in range(NCORES)]
    res = run_bass_kernel_spmd(nc, in_maps, core_ids=list(range(NCORES)))
    out = np.zeros((4, S, D), np.float32)
    for c in range(NCORES):
        b, h = c // 2, c % 2
        o = np.asarray(res.results[c]["out"]).reshape(NSTEP, 128, D)
        for i in range(NSTEP):
            qt = 2 * i + h
            out[b, 128 * qt:128 * qt + 128] = o[i]
    return out
```
